# Optimizing a Trainium2 kernel written in Bass

```python
import math
import jax, jax.numpy as jnp
from jax import lax
import numpy as np

D_MODEL = 1024
BATCH = 8
SEQ = 2048
DEPTH = 4

MEM_LEN = 256
N_MIXERS = 2
N_CONV_LAYERS = (DEPTH + 1) // 2
N_POOL_LAYERS = DEPTH // 2
N_XHEADS = 4
XHEAD_DIM = D_MODEL // N_XHEADS
D_FF = 4 * D_MODEL
CONV_WIDTH = 31
POOL_WINDOWS = (2, 4, 8, 16)
N_POOL_GROUPS = len(POOL_WINDOWS)
POOL_GROUP_DIM = D_MODEL // N_POOL_GROUPS
RMS_EPS = 1e-6
LN_EPS = 1e-5

kernel_name = "hybrid_conv_pool_memxattn_trunk"


def rmsnorm(x, g):
    xf = x.astype(jnp.float32)
    y = xf * lax.rsqrt(jnp.mean(xf * xf, axis=-1, keepdims=True) + RMS_EPS)
    return (y * g.astype(jnp.float32)).astype(x.dtype)


def layernorm(x, g, b):
    xf = x.astype(jnp.float32)
    mu = jnp.mean(xf, axis=-1, keepdims=True)
    var = jnp.mean(jnp.square(xf - mu), axis=-1, keepdims=True)
    y = (xf - mu) * lax.rsqrt(var + LN_EPS)
    return (y * g.astype(jnp.float32) + b.astype(jnp.float32)).astype(x.dtype)


def conv_mixer(h, w_in, b_in, w_dw, b_dw, ln_g, ln_b, w_out, b_out):
    u = h @ w_in + b_in
    a, gate = jnp.split(u, 2, axis=-1)
    u = a * jax.nn.sigmoid(gate)
    u = lax.conv_general_dilated(
        u, w_dw[:, None, :].astype(u.dtype),
        window_strides=(1,), padding=[(CONV_WIDTH - 1, 0)],
        dimension_numbers=("NWC", "WIO", "NWC"),
        feature_group_count=D_MODEL) + b_dw
    u = jax.nn.silu(layernorm(u, ln_g, ln_b))
    return u @ w_out + b_out


def pool_mixer(h, w_pool, scale):
    B, S, D = h.shape
    hf = h.astype(jnp.float32)
    cs = jnp.cumsum(hf, axis=1)
    count = jnp.arange(1, S + 1, dtype=jnp.float32)
    groups = []
    for g, w in enumerate(POOL_WINDOWS):
        sl = slice(g * POOL_GROUP_DIM, (g + 1) * POOL_GROUP_DIM)
        c = cs[..., sl]
        lagged = jnp.pad(c, ((0, 0), (w, 0), (0, 0)))[:, :S]
        mean = (c - lagged) / jnp.minimum(count, float(w))[None, :, None]
        groups.append(mean - hf[..., sl])
    p = jnp.stack(groups, axis=2).astype(h.dtype)
    y = jnp.einsum("bsgc,gcd->bsgd", p, w_pool).reshape(B, S, D)
    return y * scale


def mem_cross_attn(h, memn, wq, wk, wv, wo):
    B, S, D = h.shape
    q = (h @ wq).reshape(B, S, N_XHEADS, XHEAD_DIM)
    k = (memn @ wk).reshape(B, MEM_LEN, N_XHEADS, XHEAD_DIM)
    v = (memn @ wv).reshape(B, MEM_LEN, N_XHEADS, XHEAD_DIM)
    s = jnp.einsum("bshd,bmhd->bhsm", q, k).astype(jnp.float32) * (1.0 / math.sqrt(XHEAD_DIM))
    p = jax.nn.softmax(s, axis=-1).astype(v.dtype)
    o = jnp.einsum("bhsm,bmhd->bshd", p, v).reshape(B, S, D)
    return o @ wo


def sqrelu_mlp(h, w1, w2):
    return jnp.square(jax.nn.relu(h @ w1)) @ w2


def setup_inputs(seed: int = 0) -> dict:
    key = jax.random.key(seed)
    ks = jax.random.split(key, 24)
    D = D_MODEL
    nrm = lambda k, shape, fan_in: jax.random.normal(k, shape, jnp.float32) * (fan_in ** -0.5)
    gain = lambda k, shape: 1.0 + 0.05 * jax.random.normal(k, shape, jnp.float32)
    small = lambda k, shape: 0.02 * jax.random.normal(k, shape, jnp.float32)
    return {
        "x": jax.random.normal(ks[0], (BATCH, SEQ, D), jnp.float32),
        "mem": jax.random.normal(ks[1], (BATCH, MEM_LEN, D), jnp.float32),
        "mem_norm": gain(ks[2], (D,)),
        "norm_mix": gain(ks[3], (DEPTH, D)),
        "norm_xattn": gain(ks[4], (DEPTH, D)),
        "norm_mlp": gain(ks[5], (DEPTH, D)),
        "conv_w_in": nrm(ks[6], (N_CONV_LAYERS, D, 2 * D), D),
        "conv_b_in": small(ks[7], (N_CONV_LAYERS, 2 * D)),
        "conv_w_dw": nrm(ks[8], (N_CONV_LAYERS, CONV_WIDTH, D), CONV_WIDTH),
        "conv_b_dw": small(ks[9], (N_CONV_LAYERS, D)),
        "conv_ln_g": gain(ks[10], (N_CONV_LAYERS, D)),
        "conv_ln_b": small(ks[11], (N_CONV_LAYERS, D)),
        "conv_w_out": nrm(ks[12], (N_CONV_LAYERS, D, D), D),
        "conv_b_out": small(ks[13], (N_CONV_LAYERS, D)),
        "pool_w": nrm(ks[14], (N_POOL_LAYERS, N_POOL_GROUPS, POOL_GROUP_DIM, POOL_GROUP_DIM), POOL_GROUP_DIM),
        "pool_scale": gain(ks[15], (N_POOL_LAYERS, D)),
        "xattn_wq": nrm(ks[16], (DEPTH, D, D), D),
        "xattn_wk": nrm(ks[17], (DEPTH, D, D), D),
        "xattn_wv": nrm(ks[18], (DEPTH, D, D), D),
        "xattn_wo": nrm(ks[19], (DEPTH, D, D), D),
        "mlp_w1": nrm(ks[20], (DEPTH, D, D_FF), D),
        "mlp_w2": nrm(ks[21], (DEPTH, D_FF, D), D_FF),
        "final_norm": gain(ks[22], (D,)),
    }


def reference(x, mem, mem_norm, norm_mix, norm_xattn, norm_mlp,
              conv_w_in, conv_b_in, conv_w_dw, conv_b_dw, conv_ln_g, conv_ln_b,
              conv_w_out, conv_b_out, pool_w, pool_scale,
              xattn_wq, xattn_wk, xattn_wv, xattn_wo, mlp_w1, mlp_w2, final_norm):
    memn = rmsnorm(mem, mem_norm)
    for i in range(DEPTH):
        j = i // N_MIXERS
        h = rmsnorm(x, norm_mix[i])
        if i % N_MIXERS == 0:
            x = x + conv_mixer(h, conv_w_in[j], conv_b_in[j], conv_w_dw[j], conv_b_dw[j],
                               conv_ln_g[j], conv_ln_b[j], conv_w_out[j], conv_b_out[j])
        else:
            x = x + pool_mixer(h, pool_w[j], pool_scale[j])
        x = x + mem_cross_attn(rmsnorm(x, norm_xattn[i]), memn,
                               xattn_wq[i], xattn_wk[i], xattn_wv[i], xattn_wo[i])
        x = x + sqrelu_mlp(rmsnorm(x, norm_mlp[i]), mlp_w1[i], mlp_w2[i])
    return rmsnorm(x, final_norm)
```

```python
import contextlib
import numpy as np
import concourse.bass as bass
import concourse.mybir as mybir
from concourse.bass_utils import run_bass_kernel_spmd

F32 = mybir.dt.float32
BF16 = mybir.dt.bfloat16
I32 = mybir.dt.int32
ALU = mybir.AluOpType
AF = mybir.ActivationFunctionType

D = 1024
S = 2048
MEM = 256
DEPTH = 4
NCH = 8
TT = 512
NT = S // TT
CW = 31
NDV = 0
NSLOT = 6
NSTG = 1
NGEN = 6
NSTAT = 7
RMS_EPS = 1e-6
LN_EPS = 1e-5

R_NMIX, R_NXA, R_NMLP, R_BIN, R_BDW, R_LNG, R_LNB, R_BOUT, R_PSC, R_MEMN, R_FIN, R_WDW = (
    0, 4, 8, 12, 16, 18, 20, 22, 24, 26, 27, 28)
NV = 28 + 2 * CW


class Prog:
    def __init__(self):
        self.ins = []

    def op(self, eng, fn, reads=(), writes=()):
        self.ins.append(dict(eng=eng, fn=fn, reads=tuple(reads), writes=tuple(writes), dma=None))

    def dma(self, eng, fns, key, reads=(), writes=()):
        self.ins.append(dict(eng=eng, fn=list(fns), reads=tuple(reads), writes=tuple(writes), dma=key))

    def wait(self, eng, reads):
        self.ins.append(dict(eng=eng, fn=None, reads=tuple(reads), writes=(), dma=None))

    def analyze(self):
        lastw = {}
        rdrs = {}
        ins = self.ins
        for i, I in enumerate(ins):
            deps = {}
            myek = ('dma', I['dma']) if I['dma'] else I['eng']

            def add(j, kind, I=I, deps=deps):
                J = ins[j]
                ek = ('dma', J['dma']) if J['dma'] else J['eng']
                if (not J['dma']) and (not I['dma']) and J['eng'] == I['eng']:
                    if I['eng'] == 'pe' or kind == 'war':
                        return
                if deps.get(ek, -1) < j:
                    deps[ek] = j

            for r in I['reads']:
                if r in lastw:
                    add(lastw[r], 'raw')
            for w in I['writes']:
                if w in lastw:
                    add(lastw[w], 'waw')
                for ek, j in rdrs.get(w, {}).items():
                    if j != i:
                        add(j, 'war')
            I['deps'] = deps
            for r in I['reads']:
                rdrs.setdefault(r, {})[myek] = i
            for w in I['writes']:
                lastw[w] = i
                rdrs[w] = {}
        for I in ins:
            I['sig'] = False
        for I in ins:
            for ek, j in I['deps'].items():
                ins[j]['sig'] = True
        cnt = {}
        for I in ins:
            if I['dma']:
                k = ('dma', I['dma'])
                cnt[k] = cnt.get(k, 0) + 16 * len(I['fn'])
                I['cnt'] = cnt[k]
            elif I['sig']:
                k = I['eng']
                cnt[k] = cnt.get(k, 0) + 1
                I['cnt'] = cnt[k]
        self.dma_keys = sorted({I['dma'] for I in ins if I['dma']}, key=str)

    def emit(self, nc, stack):
        self.analyze()
        ins = self.ins
        sems = {}
        for e in ('pe', 'act', 'dve', 'pool', 'sp'):
            sems[e] = stack.enter_context(nc.semaphore("s_" + e))
        for k in self.dma_keys:
            sems[('dma', k)] = stack.enter_context(nc.semaphore("d_" + "_".join(str(z) for z in k)))
        block = stack.enter_context(nc.Block())

        def run(ename):
            def body(eng):
                known = {}
                for I in ins:
                    if I['eng'] != ename:
                        continue
                    for ek, j in I['deps'].items():
                        need = ins[j]['cnt']
                        if known.get(ek, 0) < need:
                            eng.wait_ge(sems[ek], need)
                            known[ek] = need
                    if I['fn'] is None:
                        continue
                    if I['dma']:
                        for f in I['fn']:
                            f(eng).then_inc(sems[('dma', I['dma'])], 16)
                    else:
                        r = I['fn'](eng)
                        if I['sig']:
                            r.then_inc(sems[ename], 1)
            return body

        block.tensor(run('pe'))
        block.scalar(run('act'))
        block.vector(run('dve'))
        block.gpsimd(run('pool'))
        block.sync(run('sp'))


def build(l0, l1, final):
    nc = bass.Bass("TRN2", target_bir_lowering=False)
    dt = lambda n, sh, kind="ExternalInput": nc.dram_tensor(n, sh, F32, kind=kind).ap()
    x_d = dt("x", [S, D])
    mem_d = dt("mem", [MEM, D])
    vec_d = {
        "norm_mix": (dt("norm_mix", [4, D]), R_NMIX), "norm_xattn": (dt("norm_xattn", [4, D]), R_NXA),
        "norm_mlp": (dt("norm_mlp", [4, D]), R_NMLP), "conv_b_in": (dt("conv_b_in", [4, D]), R_BIN),
        "conv_b_dw": (dt("conv_b_dw", [2, D]), R_BDW), "conv_ln_g": (dt("conv_ln_g", [2, D]), R_LNG),
        "conv_ln_b": (dt("conv_ln_b", [2, D]), R_LNB), "conv_b_out": (dt("conv_b_out", [2, D]), R_BOUT),
        "pool_scale": (dt("pool_scale", [2, D]), R_PSC), "mem_norm": (dt("mem_norm", [1, D]), R_MEMN),
        "final_norm": (dt("final_norm", [1, D]), R_FIN), "conv_w_dw": (dt("conv_w_dw", [2 * CW, D]), R_WDW),
    }
    w_in_d = dt("conv_w_in", [2, D, 2 * D])
    w_out_d = dt("conv_w_out", [2, D, D])
    pool_w_d = dt("pool_w", [2, 4, 256, 256])
    wq_d = dt("xattn_wq", [4, D, D])
    wk_d = dt("xattn_wk", [4, D, D])
    wv_d = dt("xattn_wv", [4, D, D])
    wo_d = dt("xattn_wo", [4, D, D])
    w1_d = dt("mlp_w1", [4, D, 4 * D])
    w2_d = dt("mlp_w2", [4, 4 * D, D])
    out_d = dt("out", [S, D], kind="ExternalOutput")

    stack = contextlib.ExitStack()
    with stack:
        sb = lambda n, sh, d: stack.enter_context(nc.sbuf_tensor(n, sh, d))[:]
        xT = sb("xT", [128, NCH, S], F32)
        wsl = sb("wsl", [128, NSLOT, 4096], BF16)
        stg = sb("stg", [128, NSTG, 1024], F32)
        gen = sb("gen", [128, NGEN, 4096], BF16)
        ubuf = sb("ubuf", [128, NCH, 544], BF16)
        statf = sb("statf", [128, NSTAT, TT], F32)
        kT = sb("kT", [128, NCH, MEM], BF16)
        Vt = sb("Vt", [128, 2, D], BF16)
        memnT = sb("memnT", [128, NCH, MEM], BF16)
        vecs = sb("vecs", [128, NCH, 128], F32)
        identf = sb("identf", [128, 128], F32)
        identb = sb("identb", [128, 128], BF16)
        onesb = sb("onesb", [128, 128], BF16)
        invc = sb("invc", [128, 4, 16], F32)
        cnt16 = sb("cnt16", [128, 16], F32)
        tmp16 = sb("tmp16", [128, 16], F32)
        epst = sb("epst", [128, 2], F32)
        epsr = epst[:, 0:1]
        epsl = epst[:, 1:2]
        psum = stack.enter_context(nc.psum_tensor("ps", [128, 8, TT], F32))[:]

        P = Prog()
        st = dict(pb=0, stg=0, gen=0, stat=0)

        def bank():
            b = st['pb'] % 8
            st['pb'] += 1
            return ('ps', b), psum[:, b, :]

        def stage():
            s = st['stg'] % NSTG
            st['stg'] += 1
            return s

        def gk(g, i=None):
            if i is None:
                return [('gen', g, q) for q in range(8)]
            return [('gen', g, i)]

        def stat():
            s = st['stat'] % 5
            st['stat'] += 1
            return ('stat', s), statf[:, s, :]

        def g3(g, n=TT):
            return gen[:, g, 0:8 * n].rearrange("p (c t) -> p c t", c=8)

        def vcol(c, v):
            return vecs[:, c, v:v + 1]

        def mm(out, lhsT, rhs, start, stop, reads, writes):
            P.op('pe', lambda e: e.matmul(out, lhsT, rhs, start=start, stop=stop), reads, writes)

        def act(out, in_, func, reads, writes, bias=None, scale=None):
            kw = {}
            if bias is not None:
                kw['bias'] = bias
            if scale is not None:
                kw['scale'] = scale
            P.op('act', lambda e: e.activation(out, in_, func, **kw), reads, writes)

        def tsc(eng, out, in0, s1, s2, op0, op1, reads, writes):
            if op1 is None:
                P.op(eng, lambda e: e.tensor_scalar(out, in0, s1, None, op0), reads, writes)
            else:
                P.op(eng, lambda e: e.tensor_scalar(out, in0, s1, s2, op0, op1), reads, writes)

        def stt(out, in0, scalar, in1, op0, op1, reads, writes):
            P.op('dve', lambda e: e.scalar_tensor_tensor(out, in0, scalar, in1, op0, op1), reads, writes)

        def tt(eng, out, in0, in1, op, reads, writes):
            P.op(eng, lambda e: e.tensor_tensor(out, in0, in1, op), reads, writes)

        def copy(eng, out, in_, reads, writes):
            if eng == 'act':
                P.op('act', lambda e: e.copy(out, in_), reads, writes)
            else:
                P.op(eng, lambda e: e.tensor_copy(out, in_), reads, writes)

        P.op('pool', lambda e: e.memset(identf, 0.0), (), [('identf',)])
        P.op('pool', lambda e: e.affine_select(out=identf, in_=identf, pattern=[[-1, 128]],
                                                compare_op=ALU.not_equal, fill=1.0, base=0,
                                                channel_multiplier=1), [('identf',)], [('identf',)])
        copy('pool', identb, identf, [('identf',)], [('identb',)])
        P.op('pool', lambda e: e.memset(onesb, 1.0), (), [('onesb',)])
        P.op('pool', lambda e: e.memset(epsr, RMS_EPS), (), [('eps',)])
        P.op('pool', lambda e: e.memset(epsl, LN_EPS), (), [('eps',)])
        P.op('pool', lambda e: e.iota(cnt16, pattern=[[1, 16]], base=1, channel_multiplier=0,
                                      allow_small_or_imprecise_dtypes=True), (), [('cnt16',)])
        for g in range(4):
            tsc('pool', invc[:, g, :], cnt16, float(2 ** (g + 1)), None, ALU.min, None,
                [('cnt16',)], [('invc', g)])
            P.op('dve', lambda e, g=g: e.reciprocal(invc[:, g, :], invc[:, g, :]), [('invc', g)], [('invc', g)])

        s0 = stage()
        fns = []
        for name, (ap, r0) in vec_d.items():
            nr = ap.shape[0]
            fns.append(lambda e, ap=ap, r0=r0, nr=nr: e.dma_start(out=stg[r0:r0 + nr, s0, :], in_=ap))
        P.dma('sp', fns, ('stg', s0), (), [('stg', s0)])
        for half in range(2):
            bk, bp = bank()
            for cc in range(4):
                c = half * 4 + cc
                P.op('pe', lambda e, c=c, cc=cc, bp=bp: e.transpose(
                    bp[:, cc * 128:cc * 128 + NV], stg[0:NV, s0, c * 128:(c + 1) * 128], identf[0:NV, 0:NV]),
                    [('stg', s0), ('identf',)], [bk])
            copy('dve', vecs[:, half * 4:half * 4 + 4, 0:NV],
                 bp.rearrange("p (c v) -> p c v", v=128)[:, :, 0:NV], [bk], [('vecs',)])

        units = []
        for l in range(l0, l1):
            jl = l // 2
            for nm, wd in (('wk', wk_d), ('wv', wv_d)):
                for q in range(2):
                    units.append(((nm, l, q), 'col', wd[l], q * 512, 4))
            if l % 2 == 0:
                for q in range(4):
                    units.append((('win', l, q), 'col', w_in_d[jl], q * 512, 4))
                for q in range(2):
                    units.append((('wout', l, q), 'col', w_out_d[jl], q * 512, 4))
            else:
                units.append((('wpool', l, 0), 'pool', pool_w_d[jl], 0, 2))
            for nm, wd in (('wq', wq_d), ('wo', wo_d)):
                for q in range(2):
                    units.append(((nm, l, q), 'col', wd[l], q * 512, 4))
            for fb in range(4):
                for q in range(2):
                    units.append((('w1', l, fb * 2 + q), 'col', w1_d[l], fb * 1024 + q * 512, 4))
                for q in range(2):
                    units.append((('w2', l, fb * 2 + q), 'row', w2_d[l], fb * 1024 + q * 512, 4))
        wm = dict(next=0, slot_of={}, free=list(range(NSLOT)))

        def load_unit(idx, slot):
            uid, kind, W, a, npc = units[idx]
            wm['slot_of'][uid] = slot
            fns = []
            for p in range(npc):
                dstp = wsl[:, slot, p * 1024:(p + 1) * 1024]
                if kind == 'col':
                    src = W[p * 256:(p + 1) * 256, a:a + 512].rearrange("(k p) n -> p k n", p=128)
                    dst = dstp.rearrange("p (k n) -> p k n", k=2)
                elif kind == 'row':
                    src = W[a + p * 128:a + (p + 1) * 128, :]
                    dst = dstp
                else:
                    src = W[2 * p:2 * p + 2].rearrange("g (kk p) d -> p (g kk) d", p=128)
                    dst = dstp.rearrange("p (k n) -> p k n", k=4)
                fns.append(lambda e, src=src, dst=dst: e.dma_start(out=dst, in_=src))
            P.dma('pool', fns, ('wd', slot), (), [('w', slot)])

        def pump():
            while wm['free'] and wm['next'] < len(units):
                load_unit(wm['next'], wm['free'].pop(0))
                wm['next'] += 1

        def wslot(uid):
            assert uid in wm['slot_of'], uid
            return wm['slot_of'][uid]

        def release(uid):
            wm['free'].append(wm['slot_of'].pop(uid))
            pump()

        def wcol(uid, k, off):
            s = wslot(uid)
            return ('w', s), wsl[:, s, :].rearrange("p (k n) -> p k n", k=8)[:, k, off:off + 128]

        pump()

        def gstage(i):
            g, hf = i // 2, i % 2
            return (gen[:, g, hf * 2048:(hf + 1) * 2048].bitcast(F32),
                    [('gen', g, q) for q in range(hf * 4, hf * 4 + 4)], ('gs', i))

        def load_rows(src_d, nrows, dst3, dkeys, bufs):
            for tt_ in range(nrows // 128):
                sap, skeys, dk = bufs[tt_ % len(bufs)]
                P.dma('sp', [lambda e, sap=sap, tt_=tt_: e.dma_start(out=sap,
                                                                     in_=src_d[tt_ * 128:(tt_ + 1) * 128, :])],
                      dk, (), skeys)
                for half in range(2):
                    bk, bp = bank()
                    for cc in range(4):
                        c = half * 4 + cc
                        P.op('pe', lambda e, c=c, cc=cc, bp=bp, sap=sap: e.transpose(
                            bp[:, cc * 128:(cc + 1) * 128], sap[:, c * 128:(c + 1) * 128], identf),
                            skeys + [('identf',)], [bk])
                    copy('act' if half == 0 else 'dve',
                         dst3(half * 4, half * 4 + 4, tt_ * 128, (tt_ + 1) * 128),
                         bp.rearrange("p (c v) -> p c v", v=128), [bk], dkeys(tt_, half))

        def rstd_from(bp, n, k1, s1, bkeys, epsap):
            act(s1[:, 0:n], bp[:, 0:n], AF.Ln, bkeys + [('eps',)], [k1], bias=epsap, scale=1.0 / D)
            act(s1[:, 0:n], s1[:, 0:n], AF.Exp, [k1], [k1], scale=-0.5)

        def rms_a(src3, srckeys, n, gs):
            act(g3(gs, n), src3, AF.Square, srckeys, gk(gs))

        def rms_b(src3, srckeys, n, grow, dst_c, dstkeys, gs):
            sq = g3(gs, n)
            bk, bp = bank()
            for c in range(NCH):
                mm(bp[:, 0:n], onesb, sq[:, c, :], c == 0, c == NCH - 1, gk(gs, c) + [('onesb',)], [bk])
            k1, s1 = stat()
            rstd_from(bp, n, k1, s1, [bk], epsr)
            for c in range(NCH):
                stt(dst_c(c), src3[:, c, :], vcol(c, grow), s1[:, 0:n], ALU.mult, ALU.mult,
                    [srckeys[c], k1, ('vecs',)], dstkeys(c))

        def rmsnorm(src3, srckeys, n, grow, dst_c, dstkeys, gs):
            rms_a(src3, srckeys, n, gs)
            rms_b(src3, srckeys, n, grow, dst_c, dstkeys, gs)

        xkeys = lambda T: [('x', c, T) for c in range(NCH)]
        xs = lambda T: xT[:, :, T * TT:(T + 1) * TT]
        xa_ = lambda j, T: xT[:, j, T * TT:(T + 1) * TT]

        load_rows(x_d, S, lambda c0, c1, t0, t1: xT[:, c0:c1, t0:t1],
                  lambda tt_, half: [('x', c, tt_ // 4) for c in range(half * 4, half * 4 + 4)],
                  [gstage(i) for i in range(2 * NGEN)])

        memT = gen[:, 0, :].bitcast(F32).rearrange("p (c t) -> p c t", c=8)
        load_rows(mem_d, MEM, lambda c0, c1, t0, t1: memT[:, c0:c1, t0:t1], lambda tt_, half: gk(0),
                  [gstage(i) for i in (2, 3)])
        rmsnorm(memT, [('gen', 0, c) for c in range(8)], MEM, R_MEMN, lambda c: memnT[:, c, :],
                lambda c: [('memnT',)], 5)

        for l in range(l0, l1):
            jl = l // 2
            AH = (0, 5)
            AQ = (1, 4)
            go, ge = 2, 3
            oT = g3(go)

            def aNa(T):
                rms_a(xs(T), xkeys(T), TT, AH[T % 2])

            def aNb(T):
                gh = AH[T % 2]
                h = g3(gh)
                rms_b(xs(T), xkeys(T), TT, R_NXA + l, lambda c: h[:, c, :], lambda c: gk(gh, c), gh)

            def aQj(T, j):
                gh = AH[T % 2]
                h = g3(gh)
                gq_ = AQ[T % 2]
                qT = g3(gq_)
                kq_, pq = bank()
                for k in range(NCH):
                    wk_, lt = wcol(('wq', l, j // 4), k, (j % 4) * 128)
                    mm(pq, lt, h[:, k, :], k == 0, k == 7, [wk_] + gk(gh, k), [kq_])
                copy('act' if j % 2 == 0 else 'dve', qT[:, j, :], pq, [kq_], gk(gq_, j))

            def emit_kv():
                for j in range(NCH):
                    kk_, pk = bank()
                    for k in range(NCH):
                        wk_, lt = wcol(('wk', l, j // 4), k, (j % 4) * 128)
                        mm(pk[:, 0:MEM], lt, memnT[:, k, :], k == 0, k == 7, [wk_, ('memnT',)], [kk_])
                    copy('act', kT[:, j, :], pk[:, 0:MEM], [kk_], [('kT',)])
                for mt in range(2):
                    for q in range(2):
                        s_ = wslot(('wv', l, q))
                        wvu = wsl[:, s_, :].rearrange("p (k n) -> p k n", k=8)
                        kv_, pv = bank()
                        for k in range(NCH):
                            mm(pv, memnT[:, k, mt * 128:(mt + 1) * 128], wvu[:, k, :], k == 0, k == 7,
                               [('w', s_), ('memnT',)], [kv_])
                        copy('dve', Vt[:, mt, q * 512:(q + 1) * 512], pv, [kv_], [('Vt',)])
                for q in range(2):
                    release(('wk', l, q))
                for q in range(2):
                    release(('wv', l, q))

            if l % 2 == 0:
                rw = R_WDW + jl * CW
                HT = (0, 5)
                gv, gz = 1, 4
                v = g3(gv)
                z = g3(gz)
                P.op('pool', lambda e: e.memset(ubuf[:, :, 0:30], 0.0), (), [('u', c) for c in range(NCH)])

                def cNa(T):
                    rms_a(xs(T), xkeys(T), TT, HT[T % 2])

                def cNb(T):
                    gh = HT[T % 2]
                    h = g3(gh)
                    rms_b(xs(T), xkeys(T), TT, R_NMIX + l, lambda c: h[:, c, :], lambda c: gk(gh, c), gh)

                def cAj(T, j):
                    gh = HT[T % 2]
                    h = g3(gh)
                    ka, pa = bank()
                    for k in range(NCH):
                        wk_, lt = wcol(('win', l, j // 4), k, (j % 4) * 128)
                        mm(pa, lt, h[:, k, :], k == 0, k == 7, [wk_] + gk(gh, k), [ka])
                    kg, pg = bank()
                    for k in range(NCH):
                        wk_, lt = wcol(('win', l, 2 + j // 4), k, (j % 4) * 128)
                        mm(pg, lt, h[:, k, :], k == 0, k == 7, [wk_] + gk(gh, k), [kg])
                    ks, sg = stat()
                    act(sg, pg, AF.Sigmoid, [kg, ('vecs',)], [ks], bias=vcol(j, R_BIN + 2 * jl + 1))
                    stt(ubuf[:, j, 30:30 + TT], pa, vcol(j, R_BIN + 2 * jl), sg, ALU.add, ALU.mult,
                        [ka, ks, ('vecs',)], [('u', j)])

                dgs = [gen[:, 2 + q, 0:CW * 128].rearrange("p (k m) -> p k m", k=CW) for q in range(2)]
                dbuilt = set()

                def ensure_diag(n):
                    if n in dbuilt or n >= NT * NCH:
                        return
                    dbuilt.add(n)
                    j, gd = n % NCH, 2 + (n % 2)
                    win = vecs[:, j, rw + NDV:rw + CW].unsqueeze(2).broadcast_to([128, CW - NDV, 128])
                    P.op('pool', lambda e, dg=dgs[n % 2], win=win: e.affine_select(
                        out=dg[:, NDV:CW, :], in_=win, pattern=[[0, CW - NDV], [-1, 128]],
                        compare_op=ALU.is_equal, fill=0.0, base=0, channel_multiplier=1), [('vecs',)], gk(gd))

                def taps(T, j):
                    if NDV == 0:
                        return None
                    ka, acc = stat()
                    tsc('dve', acc, ubuf[:, j, 0:TT], vcol(j, rw), None, ALU.mult, None,
                        [('u', j), ('vecs',)], [ka])
                    for k in range(1, NDV):
                        stt(acc, ubuf[:, j, k:k + TT], vcol(j, rw + k), acc, ALU.mult, ALU.add,
                            [('u', j), ka, ('vecs',)], [ka])
                    return ka, acc

                def cB(T, hooks):
                    cur = taps(T, 0)
                    for j in range(NCH):
                        n = T * NCH + j
                        gd = 2 + (n % 2)
                        dg = dgs[n % 2]
                        ensure_diag(n)
                        ensure_diag(n + 1)
                        nxt = taps(T, j + 1) if j + 1 < NCH else None
                        kc, pc = bank()
                        for k in range(NDV, CW):
                            mm(pc, dg[:, k, :], ubuf[:, j, k:k + TT], k == NDV, k == CW - 1,
                               gk(gd) + [('u', j)], [kc])
                        if cur is None:
                            act(v[:, j, :], pc, AF.Identity, [kc, ('vecs',)], gk(gv, j),
                                bias=vcol(j, R_BDW + jl))
                        else:
                            ka, acc = cur
                            stt(v[:, j, :], pc, vcol(j, R_BDW + jl), acc, ALU.add, ALU.add,
                                [kc, ka, ('vecs',)], gk(gv, j))
                        cur = nxt
                        act(z[:, j, :], v[:, j, :], AF.Square, gk(gv, j), gk(gz, j))
                        if T < NT - 1:
                            copy('pool', ubuf[:, j, 0:30], ubuf[:, j, TT:TT + 30], [('u', j)], [('u', j)])
                        if j in hooks:
                            hooks[j]()

                km, mean = ('stat', 5), statf[:, 5, :]
                kq, msq = ('stat', 6), statf[:, 6, :]

                def cS(T):
                    vsq = z
                    k1, p1 = bank()
                    for c in range(NCH):
                        mm(p1, onesb, v[:, c, :], c == 0, c == 7, gk(gv, c) + [('onesb',)], [k1])
                    k2, p2 = bank()
                    for c in range(NCH):
                        mm(p2, onesb, vsq[:, c, :], c == 0, c == 7, gk(gz, c) + [('onesb',)], [k2])
                    tsc('dve', mean, p1, 1.0 / D, None, ALU.mult, None, [k1], [km])
                    tt('dve', msq, mean, mean, ALU.mult, [km], [kq])
                    stt(msq, p2, 1.0 / D, msq, ALU.mult, ALU.subtract, [k2, kq], [kq])
                    act(msq, msq, AF.Ln, [kq, ('eps',)], [kq], bias=epsl)
                    act(msq, msq, AF.Exp, [kq], [kq], scale=-0.5)
                    tt('dve', mean, mean, msq, ALU.mult, [km, kq], [km])

                def cLpre(T, c, eng='pool'):
                    kt, tmp = stat()
                    tt(eng, tmp, v[:, c, :], msq, ALU.mult, gk(gv, c) + [kq], [kt])
                    tt(eng, tmp, tmp, mean, ALU.subtract, [kt, km], [kt])
                    act(tmp, tmp, AF.Identity, [kt, ('vecs',)], [kt],
                        bias=vcol(c, R_LNB + jl), scale=vcol(c, R_LNG + jl))
                    ks, sg = stat()
                    act(sg, tmp, AF.Sigmoid, [kt], [ks])
                    return kt, tmp, ks, sg

                def cLpost(T, c, pre):
                    kt, tmp, ks, sg = pre
                    tt('dve', z[:, c, :], tmp, sg, ALU.mult, [kt, ks], gk(gz, c))

                def cO(T):
                    for j in range(NCH):
                        ko, po = bank()
                        for k in range(NCH):
                            wk_, lt = wcol(('wout', l, j // 4), k, (j % 4) * 128)
                            mm(po, lt, z[:, k, :], k == 0, k == 7, [wk_] + gk(gz, k), [ko])
                        stt(xa_(j, T), po, vcol(j, R_BOUT + jl), xa_(j, T), ALU.add, ALU.add,
                            [ko, ('x', j, T), ('vecs',)], [('x', j, T)])

                ensure_diag(0)
                cNa(0)
                cNb(0)
                emit_kv()
                for j in range(NCH):
                    cAj(0, j)
                for T in range(NT):
                    if T + 1 < NT:
                        hooks = {1: (lambda T=T: cNa(T + 1)), 3: (lambda T=T: cNb(T + 1))}
                    else:
                        hooks = {1: (lambda: aNa(0)), 3: (lambda: aNb(0)), 5: (lambda: aNa(1)),
                                 6: (lambda: [release(('win', l, q)) for q in range(2)]),
                                 7: (lambda: [release(('win', l, q)) for q in range(2, 4)])}
                    cB(T, hooks)
                    cS(T)
                    for j in range(NCH):
                        pre = cLpre(T, j, 'dve')
                        if T + 1 < NT:
                            cAj(T + 1, j)
                        else:
                            aQj(0, j)
                        cLpost(T, j, pre)
                    cO(T)
                for q in range(2):
                    release(('wout', l, q))
            else:
                sp_ = wslot(('wpool', l, 0))
                wp = wsl[:, sp_, 0:2048].rearrange("p (g k d) -> p g k d", g=4, k=2)
                P.op('pool', lambda e: e.memset(ubuf[:, :, 0:16], 0.0), (), [('u', c) for c in range(NCH)])

                def pNa(T):
                    rms_a(xs(T), xkeys(T), TT, 5)

                def pNb(T):
                    rms_b(xs(T), xkeys(T), TT, R_NMIX + l, lambda c: ubuf[:, c, 16:16 + TT],
                          lambda c: [('u', c)], 5)

                def pW(T):
                    gp = T % 2
                    pp = g3(gp)
                    for c in range(NCH):
                        g = c // 2
                        w = 2 ** (g + 1)
                        kb, pb_ = bank()
                        for k in range(w):
                            mm(pb_, identb, ubuf[:, c, 16 - k:16 - k + TT], k == 0, k == w - 1,
                               [('u', c), ('identb',)], [kb])
                        stt(pp[:, c, :], pb_, 1.0 / w, ubuf[:, c, 16:16 + TT], ALU.mult, ALU.subtract,
                            [kb, ('u', c)], gk(gp, c))
                        if T == 0:
                            tt('dve', tmp16, pb_[:, 0:16], invc[:, g, :], ALU.mult, [kb, ('invc', g)], [('tmp16',)])
                            tt('dve', pp[:, c, 0:16], tmp16, ubuf[:, c, 16:32], ALU.subtract,
                               [('tmp16',), ('u', c)], gk(gp, c))
                        if T < NT - 1:
                            copy('pool', ubuf[:, c, 0:16], ubuf[:, c, TT:TT + 16], [('u', c)], [('u', c)])

                def pJ(T):
                    gp = T % 2
                    pp = g3(gp)
                    for j in range(NCH):
                        g = j // 2
                        ky, py = bank()
                        for kk in range(2):
                            mm(py, wp[:, g, kk, (j % 2) * 128:(j % 2) * 128 + 128], pp[:, 2 * g + kk, :],
                               kk == 0, kk == 1, [('w', sp_)] + gk(gp, 2 * g + kk), [ky])
                        stt(xa_(j, T), py, vcol(j, R_PSC + jl), xa_(j, T), ALU.mult, ALU.add,
                            [ky, ('x', j, T), ('vecs',)], [('x', j, T)])

                pNa(0)
                pNb(0)
                emit_kv()
                for T in range(NT):
                    if T + 1 < NT:
                        pNa(T + 1)
                    pW(T)
                    if T + 1 < NT:
                        pNb(T + 1)
                    else:
                        aNa(0)
                        aNb(0)
                    pJ(T)
                aNa(1)
                release(('wpool', l, 0))


            def mNa(T):
                rms_a(xs(T), xkeys(T), TT, T)

            def mNb(T):
                hh = g3(T)
                rms_b(xs(T), xkeys(T), TT, R_NMLP + l, lambda c: hh[:, c, :], lambda c: gk(T, c), T)

            def aQ(T):
                for j in range(NCH):
                    aQj(T, j)

            def aH(T):
                gq_ = AQ[T % 2]
                qT = g3(gq_)
                Es = [gen[:, ge, hd * 1024:(hd + 1) * 1024].rearrange("p (m t) -> p m t", m=2) for hd in range(4)]

                def scores(hd):
                    for mt in range(2):
                        ks_, ps_ = bank()
                        for dd in range(2):
                            mm(ps_, kT[:, 2 * hd + dd, mt * 128:(mt + 1) * 128], qT[:, 2 * hd + dd, :],
                               dd == 0, dd == 1, [('kT',)] + gk(gq_, 2 * hd + dd), [ks_])
                        act(Es[hd][:, mt, :], ps_, AF.Exp, [ks_], gk(ge, 2 * hd + mt), scale=1.0 / 16.0)

                def pv(hd):
                    E = Es[hd]
                    kz, pz = bank()
                    for mt in range(2):
                        mm(pz, onesb, E[:, mt, :], mt == 0, mt == 1, gk(ge, 2 * hd + mt) + [('onesb',)], [kz])
                    krz, rz = stat()
                    act(rz, pz, AF.Ln, [kz], [krz])
                    act(rz, rz, AF.Exp, [krz], [krz], scale=-1.0)
                    for dd in range(2):
                        ko_, po_ = bank()
                        for mt in range(2):
                            mm(po_, Vt[:, mt, (2 * hd + dd) * 128:(2 * hd + dd + 1) * 128], E[:, mt, :],
                               mt == 0, mt == 1, [('Vt',)] + gk(ge, 2 * hd + mt), [ko_])
                        tt('dve', oT[:, 2 * hd + dd, :], po_, rz, ALU.mult, [ko_, krz], gk(go, 2 * hd + dd))

                scores(0)
                for hd in range(4):
                    if hd + 1 < 4:
                        scores(hd + 1)
                    pv(hd)

            def aW(T):
                for j in range(NCH):
                    kw_, pw = bank()
                    for k in range(NCH):
                        wk_, lt = wcol(('wo', l, j // 4), k, (j % 4) * 128)
                        mm(pw, lt, oT[:, k, :], k == 0, k == 7, [wk_] + gk(go, k), [kw_])
                    tt('dve', xa_(j, T), pw, xa_(j, T), ALU.add, [kw_, ('x', j, T)], [('x', j, T)])

            if l % 2 == 1:
                aQ(0)
            aNb(1)
            for T in range(NT):
                aH(T)
                if T + 1 < NT:
                    aQ(T + 1)
                else:
                    for q in range(2):
                        release(('wq', l, q))
                if T + 2 < NT:
                    aNa(T + 2)
                if T == NT - 1:
                    mNa(0)
                    mNa(1)
                aW(T)
                if T + 2 < NT:
                    aNb(T + 2)
            for q in range(2):
                release(('wo', l, q))

            def mU(s):
                fb, T = s // NT, s % NT
                hh = g3(T)
                g1 = 4 + (s % 2)
                h1 = g3(g1)
                for i in range(NCH):
                    k1_, p1_ = bank()
                    for k in range(NCH):
                        wk_, lt = wcol(('w1', l, fb * 2 + i // 4), k, (i % 4) * 128)
                        mm(p1_, lt, hh[:, k, :], k == 0, k == 7, [wk_] + gk(T, k), [k1_])
                    if i % 2 == 0:
                        act(h1[:, i, :], p1_, AF.Relu, [k1_], gk(g1, i))
                    else:
                        tsc('dve', h1[:, i, :], p1_, 0.0, None, ALU.max, None, [k1_], gk(g1, i))
                    tt('pool', h1[:, i, :], h1[:, i, :], h1[:, i, :], ALU.mult, gk(g1, i), gk(g1, i))
                if T == NT - 1:
                    for q in range(2):
                        release(('w1', l, fb * 2 + q))

            def mD(s):
                fb, T = s // NT, s % NT
                g1 = 4 + (s % 2)
                h1 = g3(g1)
                for j in range(NCH):
                    k2_, p2_ = bank()
                    for i in range(NCH):
                        s_ = wslot(('w2', l, fb * 2 + i // 4))
                        w2u = wsl[:, s_, :].rearrange("p (k n) -> p k n", k=4)
                        mm(p2_, w2u[:, i % 4, j * 128:(j + 1) * 128], h1[:, i, :], i == 0, i == 7,
                           [('w', s_)] + gk(g1, i), [k2_])
                    tt('dve', xa_(j, T), p2_, xa_(j, T), ALU.add, [k2_, ('x', j, T)], [('x', j, T)])
                if T == NT - 1:
                    for q in range(2):
                        release(('w2', l, fb * 2 + q))

            mNb(0)
            mNa(2)
            mNb(1)
            mNa(3)
            mU(0)
            mNb(2)
            mNb(3)
            for s in range(4 * NT):
                if s + 1 < 4 * NT:
                    mU(s + 1)
                mD(s)

        def fNa(T):
            act(g3(5), xs(T), AF.Square, xkeys(T), gk(5))

        def fNb(T):
            sq = g3(5)
            bk, bp = bank()
            for c in range(NCH):
                mm(bp, onesb, sq[:, c, :], c == 0, c == 7, gk(5, c) + [('onesb',)], [bk])
            k1, s1 = stat()
            rstd_from(bp, TT, k1, s1, [bk], epsr)
            for c in range(NCH):
                stt(xa_(c, T), xa_(c, T), vcol(c, R_FIN), s1, ALU.mult, ALU.mult,
                    [('x', c, T), k1, ('vecs',)], [('x', c, T)])

        nout = [0]

        def fO(T):
            for s4 in range(4):
                t0 = T * TT + s4 * 128
                sap, skeys, dk = gstage(nout[0] % 10)
                for half in range(2):
                    bk, bp = bank()
                    for cc in range(4):
                        c = half * 4 + cc
                        P.op('pe', lambda e, c=c, cc=cc, bp=bp, t0=t0: e.transpose(
                            bp[:, cc * 128:(cc + 1) * 128], xT[:, c, t0:t0 + 128], identf),
                            [('x', c, T), ('identf',)], [bk])
                    copy('act' if half == 0 else 'dve', sap[:, half * 512:(half + 1) * 512], bp,
                         [bk], skeys[half * 2:half * 2 + 2])
                P.dma('sp', [lambda e, sap=sap, t0=t0: e.dma_start(out=out_d[t0:t0 + 128, :], in_=sap)],
                      dk, skeys, [('out', nout[0])])
                nout[0] += 1

        if final:
            fNa(0)
            fNb(0)
            for T in range(NT):
                if T + 1 < NT:
                    fNa(T + 1)
                fO(T)
                if T + 1 < NT:
                    fNb(T + 1)
        else:
            for T in range(NT):
                fO(T)
        nout = nout[0]
        P.wait('sp', [('out', i) for i in range(nout)])
        P.emit(nc, stack)
    return nc


LAUNCH_GROUPS = [(0, 4)]
_CACHE = {}


def _prog(l0, l1, final):
    key = (l0, l1, final)
    if key not in _CACHE:
        _CACHE[key] = build(l0, l1, final)
    return _CACHE[key]


def kernel(**inp):
    f = lambda a: np.ascontiguousarray(np.asarray(a, dtype=np.float32))
    B = 8
    x = f(inp["x"])
    mem = f(inp["mem"])
    shared = {
        "norm_mix": f(inp["norm_mix"]), "norm_xattn": f(inp["norm_xattn"]), "norm_mlp": f(inp["norm_mlp"]),
        "conv_b_in": f(inp["conv_b_in"]).reshape(4, D), "conv_b_dw": f(inp["conv_b_dw"]),
        "conv_ln_g": f(inp["conv_ln_g"]), "conv_ln_b": f(inp["conv_ln_b"]), "conv_b_out": f(inp["conv_b_out"]),
        "pool_scale": f(inp["pool_scale"]), "mem_norm": f(inp["mem_norm"]).reshape(1, D),
        "final_norm": f(inp["final_norm"]).reshape(1, D), "conv_w_dw": f(inp["conv_w_dw"]).reshape(2 * CW, D),
        "conv_w_in": f(inp["conv_w_in"]), "conv_w_out": f(inp["conv_w_out"]), "pool_w": f(inp["pool_w"]),
        "xattn_wq": f(inp["xattn_wq"]), "xattn_wk": f(inp["xattn_wk"]), "xattn_wv": f(inp["xattn_wv"]),
        "xattn_wo": f(inp["xattn_wo"]), "mlp_w1": f(inp["mlp_w1"]), "mlp_w2": f(inp["mlp_w2"]),
    }
    cur = [x[b] for b in range(B)]
    for (l0, l1) in LAUNCH_GROUPS:
        nc = _prog(l0, l1, l1 == DEPTH)
        in_maps = [dict(shared, x=cur[b], mem=mem[b]) for b in range(B)]
        res = run_bass_kernel_spmd(nc, in_maps, core_ids=list(range(B)))
        cur = [np.asarray(res.results[b]["out"], dtype=np.float32) for b in range(B)]
    return np.stack(cur, axis=0)
```

```python
import contextlib
import numpy as np
import concourse.bass as bass
import concourse.mybir as mybir
from concourse.bass_utils import run_bass_kernel_spmd

F32 = mybir.dt.float32
BF16 = mybir.dt.bfloat16
I32 = mybir.dt.int32
ALU = mybir.AluOpType
AF = mybir.ActivationFunctionType

D = 1024
S = 2048
MEM = 256
DEPTH = 4
NCH = 8
TT = 512
NT = S // TT
CW = 31
NDV = 0
NSLOT = 6
NSTG = 1
NGEN = 6
NSTAT = 7
RMS_EPS = 1e-6
LN_EPS = 1e-5

R_NMIX, R_NXA, R_NMLP, R_BIN, R_BDW, R_LNG, R_LNB, R_BOUT, R_PSC, R_MEMN, R_FIN, R_WDW = (
    0, 4, 8, 12, 16, 18, 20, 22, 24, 26, 27, 28)
NV = 28 + 2 * CW


class Prog:
    def __init__(self):
        self.ins = []

    def op(self, eng, fn, reads=(), writes=()):
        self.ins.append(dict(eng=eng, fn=fn, reads=tuple(reads), writes=tuple(writes), dma=None))

    def dma(self, eng, fns, key, reads=(), writes=()):
        self.ins.append(dict(eng=eng, fn=list(fns), reads=tuple(reads), writes=tuple(writes), dma=key))

    def wait(self, eng, reads):
        self.ins.append(dict(eng=eng, fn=None, reads=tuple(reads), writes=(), dma=None))

    def analyze(self):
        lastw = {}
        rdrs = {}
        ins = self.ins
        for i, I in enumerate(ins):
            deps = {}
            myek = ('dma', I['dma']) if I['dma'] else I['eng']

            def add(j, kind, I=I, deps=deps):
                J = ins[j]
                ek = ('dma', J['dma']) if J['dma'] else J['eng']
                if (not J['dma']) and (not I['dma']) and J['eng'] == I['eng']:
                    if I['eng'] == 'pe':
                        return
                if deps.get(ek, -1) < j:
                    deps[ek] = j

            for r in I['reads']:
                if r in lastw:
                    add(lastw[r], 'raw')
            for w in I['writes']:
                if w in lastw:
                    add(lastw[w], 'waw')
                for ek, j in rdrs.get(w, {}).items():
                    if j != i:
                        add(j, 'war')
            I['deps'] = deps
            for r in I['reads']:
                rdrs.setdefault(r, {})[myek] = i
            for w in I['writes']:
                lastw[w] = i
                rdrs[w] = {}
        for I in ins:
            I['sig'] = False
        for I in ins:
            for ek, j in I['deps'].items():
                ins[j]['sig'] = True
        cnt = {}
        for I in ins:
            if I['dma']:
                k = ('dma', I['dma'])
                cnt[k] = cnt.get(k, 0) + 16 * len(I['fn'])
                I['cnt'] = cnt[k]
            elif I['sig']:
                k = I['eng']
                cnt[k] = cnt.get(k, 0) + 1
                I['cnt'] = cnt[k]
        self.dma_keys = sorted({I['dma'] for I in ins if I['dma']}, key=str)

    def emit(self, nc, stack):
        self.analyze()
        ins = self.ins
        sems = {}
        for e in ('pe', 'act', 'dve', 'pool', 'sp'):
            sems[e] = stack.enter_context(nc.semaphore("s_" + e))
        for k in self.dma_keys:
            sems[('dma', k)] = stack.enter_context(nc.semaphore("d_" + "_".join(str(z) for z in k)))
        block = stack.enter_context(nc.Block())

        def run(ename):
            def body(eng):
                known = {}
                for I in ins:
                    if I['eng'] != ename:
                        continue
                    for ek, j in I['deps'].items():
                        need = ins[j]['cnt']
                        if known.get(ek, 0) < need:
                            eng.wait_ge(sems[ek], need)
                            known[ek] = need
                    if I['fn'] is None:
                        continue
                    if I['dma']:
                        for f in I['fn']:
                            f(eng).then_inc(sems[('dma', I['dma'])], 16)
                    else:
                        r = I['fn'](eng)
                        if I['sig']:
                            r.then_inc(sems[ename], 1)
            return body

        block.tensor(run('pe'))
        block.scalar(run('act'))
        block.vector(run('dve'))
        block.gpsimd(run('pool'))
        block.sync(run('sp'))


def build(l0, l1, final):
    nc = bass.Bass("TRN2", target_bir_lowering=False)
    dt = lambda n, sh, kind="ExternalInput": nc.dram_tensor(n, sh, F32, kind=kind).ap()
    x_d = dt("x", [S, D])
    mem_d = dt("mem", [MEM, D])
    vec_d = {
        "norm_mix": (dt("norm_mix", [4, D]), R_NMIX), "norm_xattn": (dt("norm_xattn", [4, D]), R_NXA),
        "norm_mlp": (dt("norm_mlp", [4, D]), R_NMLP), "conv_b_in": (dt("conv_b_in", [4, D]), R_BIN),
        "conv_b_dw": (dt("conv_b_dw", [2, D]), R_BDW), "conv_ln_g": (dt("conv_ln_g", [2, D]), R_LNG),
        "conv_ln_b": (dt("conv_ln_b", [2, D]), R_LNB), "conv_b_out": (dt("conv_b_out", [2, D]), R_BOUT),
        "pool_scale": (dt("pool_scale", [2, D]), R_PSC), "mem_norm": (dt("mem_norm", [1, D]), R_MEMN),
        "final_norm": (dt("final_norm", [1, D]), R_FIN), "conv_w_dw": (dt("conv_w_dw", [2 * CW, D]), R_WDW),
    }
    w_in_d = dt("conv_w_in", [2, D, 2 * D])
    w_out_d = dt("conv_w_out", [2, D, D])
    pool_w_d = dt("pool_w", [2, 4, 256, 256])
    wq_d = dt("xattn_wq", [4, D, D])
    wk_d = dt("xattn_wk", [4, D, D])
    wv_d = dt("xattn_wv", [4, D, D])
    wo_d = dt("xattn_wo", [4, D, D])
    w1_d = dt("mlp_w1", [4, D, 4 * D])
    w2_d = dt("mlp_w2", [4, 4 * D, D])
    out_d = dt("out", [S, D], kind="ExternalOutput")

    stack = contextlib.ExitStack()
    with stack:
        sb = lambda n, sh, d: stack.enter_context(nc.sbuf_tensor(n, sh, d))[:]
        xT = sb("xT", [128, NCH, S], F32)
        wsl = sb("wsl", [128, NSLOT, 4096], BF16)
        stg = sb("stg", [128, NSTG, 1024], F32)
        gen = sb("gen", [128, NGEN, 4096], BF16)
        ubuf = sb("ubuf", [128, NCH, 544], BF16)
        statf = sb("statf", [128, NSTAT, TT], F32)
        kT = sb("kT", [128, NCH, MEM], BF16)
        Vt = sb("Vt", [128, 2, D], BF16)
        memnT = sb("memnT", [128, NCH, MEM], BF16)
        vecs = sb("vecs", [128, NCH, 128], F32)
        identf = sb("identf", [128, 128], F32)
        identb = sb("identb", [128, 128], BF16)
        onesb = sb("onesb", [128, 128], BF16)
        invc = sb("invc", [128, 4, 16], F32)
        cnt16 = sb("cnt16", [128, 16], F32)
        tmp16 = sb("tmp16", [128, 16], F32)
        epst = sb("epst", [128, 2], F32)
        epsr = epst[:, 0:1]
        epsl = epst[:, 1:2]
        psum = stack.enter_context(nc.psum_tensor("ps", [128, 8, TT], F32))[:]

        P = Prog()
        st = dict(pb=0, stg=0, gen=0, stat=0)

        def bank():
            b = st['pb'] % 8
            st['pb'] += 1
            return ('ps', b), psum[:, b, :]

        def stage():
            s = st['stg'] % NSTG
            st['stg'] += 1
            return s

        def gk(g, i=None):
            if i is None:
                return [('gen', g, q) for q in range(8)]
            return [('gen', g, i)]

        def stat():
            s = st['stat'] % 5
            st['stat'] += 1
            return ('stat', s), statf[:, s, :]

        def g3(g, n=TT):
            return gen[:, g, 0:8 * n].rearrange("p (c t) -> p c t", c=8)

        def vcol(c, v):
            return vecs[:, c, v:v + 1]

        def mm(out, lhsT, rhs, start, stop, reads, writes):
            P.op('pe', lambda e: e.matmul(out, lhsT, rhs, start=start, stop=stop), reads, writes)

        def act(out, in_, func, reads, writes, bias=None, scale=None):
            kw = {}
            if bias is not None:
                kw['bias'] = bias
            if scale is not None:
                kw['scale'] = scale
            P.op('act', lambda e: e.activation(out, in_, func, **kw), reads, writes)

        def tsc(eng, out, in0, s1, s2, op0, op1, reads, writes):
            if op1 is None:
                P.op(eng, lambda e: e.tensor_scalar(out, in0, s1, None, op0), reads, writes)
            else:
                P.op(eng, lambda e: e.tensor_scalar(out, in0, s1, s2, op0, op1), reads, writes)

        def stt(out, in0, scalar, in1, op0, op1, reads, writes):
            P.op('dve', lambda e: e.scalar_tensor_tensor(out, in0, scalar, in1, op0, op1), reads, writes)

        def tt(eng, out, in0, in1, op, reads, writes):
            P.op(eng, lambda e: e.tensor_tensor(out, in0, in1, op), reads, writes)

        def copy(eng, out, in_, reads, writes):
            if eng == 'act':
                P.op('act', lambda e: e.copy(out, in_), reads, writes)
            else:
                P.op(eng, lambda e: e.tensor_copy(out, in_), reads, writes)

        P.op('pool', lambda e: e.memset(identf, 0.0), (), [('identf',)])
        P.op('pool', lambda e: e.affine_select(out=identf, in_=identf, pattern=[[-1, 128]],
                                                compare_op=ALU.not_equal, fill=1.0, base=0,
                                                channel_multiplier=1), [('identf',)], [('identf',)])
        copy('pool', identb, identf, [('identf',)], [('identb',)])
        P.op('pool', lambda e: e.memset(onesb, 1.0), (), [('onesb',)])
        P.op('pool', lambda e: e.memset(epsr, RMS_EPS), (), [('eps',)])
        P.op('pool', lambda e: e.memset(epsl, LN_EPS), (), [('eps',)])
        P.op('pool', lambda e: e.iota(cnt16, pattern=[[1, 16]], base=1, channel_multiplier=0,
                                      allow_small_or_imprecise_dtypes=True), (), [('cnt16',)])
        for g in range(4):
            tsc('pool', invc[:, g, :], cnt16, float(2 ** (g + 1)), None, ALU.min, None,
                [('cnt16',)], [('invc', g)])
            P.op('dve', lambda e, g=g: e.reciprocal(invc[:, g, :], invc[:, g, :]), [('invc', g)], [('invc', g)])

        s0 = stage()
        fns = []
        for name, (ap, r0) in vec_d.items():
            nr = ap.shape[0]
            fns.append(lambda e, ap=ap, r0=r0, nr=nr: e.dma_start(out=stg[r0:r0 + nr, s0, :], in_=ap))
        P.dma('sp', fns, ('stg', s0), (), [('stg', s0)])
        for half in range(2):
            bk, bp = bank()
            for cc in range(4):
                c = half * 4 + cc
                P.op('pe', lambda e, c=c, cc=cc, bp=bp: e.transpose(
                    bp[:, cc * 128:cc * 128 + NV], stg[0:NV, s0, c * 128:(c + 1) * 128], identf[0:NV, 0:NV]),
                    [('stg', s0), ('identf',)], [bk])
            copy('dve', vecs[:, half * 4:half * 4 + 4, 0:NV],
                 bp.rearrange("p (c v) -> p c v", v=128)[:, :, 0:NV], [bk], [('vecs',)])

        units = []
        for l in range(l0, l1):
            jl = l // 2
            for nm, wd in (('wk', wk_d), ('wv', wv_d)):
                for q in range(2):
                    units.append(((nm, l, q), 'col', wd[l], q * 512, 4))
            if l % 2 == 0:
                for q in range(4):
                    units.append((('win', l, q), 'col', w_in_d[jl], q * 512, 4))
                for q in range(2):
                    units.append((('wout', l, q), 'col', w_out_d[jl], q * 512, 4))
            else:
                units.append((('wpool', l, 0), 'pool', pool_w_d[jl], 0, 2))
            for nm, wd in (('wq', wq_d), ('wo', wo_d)):
                for q in range(2):
                    units.append(((nm, l, q), 'col', wd[l], q * 512, 4))
            for fb in range(4):
                for q in range(2):
                    units.append((('w1', l, fb * 2 + q), 'col', w1_d[l], fb * 1024 + q * 512, 4))
                for q in range(2):
                    units.append((('w2', l, fb * 2 + q), 'row', w2_d[l], fb * 1024 + q * 512, 4))
        wm = dict(next=0, slot_of={}, free=list(range(NSLOT)))

        def load_unit(idx, slot):
            uid, kind, W, a, npc = units[idx]
            wm['slot_of'][uid] = slot
            fns = []
            for p in range(npc):
                dstp = wsl[:, slot, p * 1024:(p + 1) * 1024]
                if kind == 'col':
                    src = W[p * 256:(p + 1) * 256, a:a + 512].rearrange("(k p) n -> p k n", p=128)
                    dst = dstp.rearrange("p (k n) -> p k n", k=2)
                elif kind == 'row':
                    src = W[a + p * 128:a + (p + 1) * 128, :]
                    dst = dstp
                else:
                    src = W[2 * p:2 * p + 2].rearrange("g (kk p) d -> p (g kk) d", p=128)
                    dst = dstp.rearrange("p (k n) -> p k n", k=4)
                fns.append(lambda e, src=src, dst=dst: e.dma_start(out=dst, in_=src))
            P.dma('pool', fns, ('wd', slot), (), [('w', slot)])

        def pump():
            while wm['free'] and wm['next'] < len(units):
                load_unit(wm['next'], wm['free'].pop(0))
                wm['next'] += 1

        def wslot(uid):
            assert uid in wm['slot_of'], uid
            return wm['slot_of'][uid]

        def release(uid):
            wm['free'].append(wm['slot_of'].pop(uid))
            pump()

        def wcol(uid, k, off):
            s = wslot(uid)
            return ('w', s), wsl[:, s, :].rearrange("p (k n) -> p k n", k=8)[:, k, off:off + 128]

        pump()

        def gstage(i):
            g, hf = i // 2, i % 2
            return (gen[:, g, hf * 2048:(hf + 1) * 2048].bitcast(F32),
                    [('gen', g, q) for q in range(hf * 4, hf * 4 + 4)], ('gs', i))

        def load_rows(src_d, nrows, dst3, dkeys, bufs):
            for tt_ in range(nrows // 128):
                sap, skeys, dk = bufs[tt_ % len(bufs)]
                P.dma('sp', [lambda e, sap=sap, tt_=tt_: e.dma_start(out=sap,
                                                                     in_=src_d[tt_ * 128:(tt_ + 1) * 128, :])],
                      dk, (), skeys)
                for half in range(2):
                    bk, bp = bank()
                    for cc in range(4):
                        c = half * 4 + cc
                        P.op('pe', lambda e, c=c, cc=cc, bp=bp, sap=sap: e.transpose(
                            bp[:, cc * 128:(cc + 1) * 128], sap[:, c * 128:(c + 1) * 128], identf),
                            skeys + [('identf',)], [bk])
                    copy('act' if half == 0 else 'dve',
                         dst3(half * 4, half * 4 + 4, tt_ * 128, (tt_ + 1) * 128),
                         bp.rearrange("p (c v) -> p c v", v=128), [bk], dkeys(tt_, half))

        def rstd_from(bp, n, k1, s1, bkeys, epsap):
            act(s1[:, 0:n], bp[:, 0:n], AF.Ln, bkeys + [('eps',)], [k1], bias=epsap, scale=1.0 / D)
            act(s1[:, 0:n], s1[:, 0:n], AF.Exp, [k1], [k1], scale=-0.5)

        def rms_a(src3, srckeys, n, gs):
            act(g3(gs, n), src3, AF.Square, srckeys, gk(gs))

        def rms_b1(n, gs):
            sq = g3(gs, n)
            bk, bp = bank()
            for c in range(NCH):
                mm(bp[:, 0:n], onesb, sq[:, c, :], c == 0, c == NCH - 1, gk(gs, c) + [('onesb',)], [bk])
            k1, s1 = stat()
            rstd_from(bp, n, k1, s1, [bk], epsr)
            return k1, s1

        def rms_b2(src3, srckeys, n, grow, dst_c, dstkeys, ks):
            k1, s1 = ks
            for c in range(NCH):
                stt(dst_c(c), src3[:, c, :], vcol(c, grow), s1[:, 0:n], ALU.mult, ALU.mult,
                    [srckeys[c], k1, ('vecs',)], dstkeys(c))

        def rms_b(src3, srckeys, n, grow, dst_c, dstkeys, gs):
            rms_b2(src3, srckeys, n, grow, dst_c, dstkeys, rms_b1(n, gs))

        def rmsnorm(src3, srckeys, n, grow, dst_c, dstkeys, gs):
            rms_a(src3, srckeys, n, gs)
            rms_b(src3, srckeys, n, grow, dst_c, dstkeys, gs)

        xkeys = lambda T: [('x', c, T) for c in range(NCH)]
        xs = lambda T: xT[:, :, T * TT:(T + 1) * TT]
        xa_ = lambda j, T: xT[:, j, T * TT:(T + 1) * TT]

        load_rows(x_d, S, lambda c0, c1, t0, t1: xT[:, c0:c1, t0:t1],
                  lambda tt_, half: [('x', c, tt_ // 4) for c in range(half * 4, half * 4 + 4)],
                  [gstage(i) for i in range(2 * NGEN)])

        memT = gen[:, 0, :].bitcast(F32).rearrange("p (c t) -> p c t", c=8)
        load_rows(mem_d, MEM, lambda c0, c1, t0, t1: memT[:, c0:c1, t0:t1], lambda tt_, half: gk(0),
                  [gstage(i) for i in (2, 3)])
        rmsnorm(memT, [('gen', 0, c) for c in range(8)], MEM, R_MEMN, lambda c: memnT[:, c, :],
                lambda c: [('memnT',)], 5)

        for l in range(l0, l1):
            jl = l // 2
            AH = (0, 5)
            AQ = (1, 4)
            go, ge = 2, 3
            oT = g3(go)

            def aNa(T):
                rms_a(xs(T), xkeys(T), TT, AH[T % 2])

            def aNb(T):
                gh = AH[T % 2]
                h = g3(gh)
                rms_b(xs(T), xkeys(T), TT, R_NXA + l, lambda c: h[:, c, :], lambda c: gk(gh, c), gh)

            def aQj(T, j):
                gh = AH[T % 2]
                h = g3(gh)
                gq_ = AQ[T % 2]
                qT = g3(gq_)
                kq_, pq = bank()
                for k in range(NCH):
                    wk_, lt = wcol(('wq', l, j // 4), k, (j % 4) * 128)
                    mm(pq, lt, h[:, k, :], k == 0, k == 7, [wk_] + gk(gh, k), [kq_])
                copy('act' if j % 2 == 0 else 'dve', qT[:, j, :], pq, [kq_], gk(gq_, j))

            def emit_kv():
                for j in range(NCH):
                    kk_, pk = bank()
                    for k in range(NCH):
                        wk_, lt = wcol(('wk', l, j // 4), k, (j % 4) * 128)
                        mm(pk[:, 0:MEM], lt, memnT[:, k, :], k == 0, k == 7, [wk_, ('memnT',)], [kk_])
                    copy('act', kT[:, j, :], pk[:, 0:MEM], [kk_], [('kT',)])
                for mt in range(2):
                    for q in range(2):
                        s_ = wslot(('wv', l, q))
                        wvu = wsl[:, s_, :].rearrange("p (k n) -> p k n", k=8)
                        kv_, pv = bank()
                        for k in range(NCH):
                            mm(pv, memnT[:, k, mt * 128:(mt + 1) * 128], wvu[:, k, :], k == 0, k == 7,
                               [('w', s_), ('memnT',)], [kv_])
                        copy('dve', Vt[:, mt, q * 512:(q + 1) * 512], pv, [kv_], [('Vt',)])
                for q in range(2):
                    release(('wk', l, q))
                for q in range(2):
                    release(('wv', l, q))

            if l % 2 == 0:
                rw = R_WDW + jl * CW
                HT = (0, 5)
                gv, gz = 1, 4
                v = g3(gv)
                z = g3(gz)
                P.op('pool', lambda e: e.memset(ubuf[:, :, 0:30], 0.0), (), [('u', c) for c in range(NCH)])

                def cNa(T):
                    rms_a(xs(T), xkeys(T), TT, HT[T % 2])

                def cNb(T):
                    gh = HT[T % 2]
                    h = g3(gh)
                    rms_b(xs(T), xkeys(T), TT, R_NMIX + l, lambda c: h[:, c, :], lambda c: gk(gh, c), gh)

                def cAj(T, j):
                    gh = HT[T % 2]
                    h = g3(gh)
                    ka, pa = bank()
                    for k in range(NCH):
                        wk_, lt = wcol(('win', l, j // 4), k, (j % 4) * 128)
                        mm(pa, lt, h[:, k, :], k == 0, k == 7, [wk_] + gk(gh, k), [ka])
                    kg, pg = bank()
                    for k in range(NCH):
                        wk_, lt = wcol(('win', l, 2 + j // 4), k, (j % 4) * 128)
                        mm(pg, lt, h[:, k, :], k == 0, k == 7, [wk_] + gk(gh, k), [kg])
                    ks, sg = stat()
                    act(sg, pg, AF.Sigmoid, [kg, ('vecs',)], [ks], bias=vcol(j, R_BIN + 2 * jl + 1))
                    stt(ubuf[:, j, 30:30 + TT], pa, vcol(j, R_BIN + 2 * jl), sg, ALU.add, ALU.mult,
                        [ka, ks, ('vecs',)], [('u', j)])

                dgs = [gen[:, 2 + q, 0:CW * 128].rearrange("p (k m) -> p k m", k=CW) for q in range(2)]
                dbuilt = set()

                def ensure_diag(n):
                    if n in dbuilt or n >= NT * NCH:
                        return
                    dbuilt.add(n)
                    j, gd = n % NCH, 2 + (n % 2)
                    win = vecs[:, j, rw + NDV:rw + CW].unsqueeze(2).broadcast_to([128, CW - NDV, 128])
                    P.op('pool', lambda e, dg=dgs[n % 2], win=win: e.affine_select(
                        out=dg[:, NDV:CW, :], in_=win, pattern=[[0, CW - NDV], [-1, 128]],
                        compare_op=ALU.is_equal, fill=0.0, base=0, channel_multiplier=1), [('vecs',)], gk(gd))

                def taps(T, j):
                    if NDV == 0:
                        return None
                    ka, acc = stat()
                    tsc('dve', acc, ubuf[:, j, 0:TT], vcol(j, rw), None, ALU.mult, None,
                        [('u', j), ('vecs',)], [ka])
                    for k in range(1, NDV):
                        stt(acc, ubuf[:, j, k:k + TT], vcol(j, rw + k), acc, ALU.mult, ALU.add,
                            [('u', j), ka, ('vecs',)], [ka])
                    return ka, acc

                def cB(T, hooks):
                    cur = taps(T, 0)
                    for j in range(NCH):
                        n = T * NCH + j
                        gd = 2 + (n % 2)
                        dg = dgs[n % 2]
                        ensure_diag(n)
                        ensure_diag(n + 1)
                        nxt = taps(T, j + 1) if j + 1 < NCH else None
                        kc, pc = bank()
                        for k in range(NDV, CW):
                            mm(pc, dg[:, k, :], ubuf[:, j, k:k + TT], k == NDV, k == CW - 1,
                               gk(gd) + [('u', j)], [kc])
                        if cur is None:
                            act(v[:, j, :], pc, AF.Identity, [kc, ('vecs',)], gk(gv, j),
                                bias=vcol(j, R_BDW + jl))
                        else:
                            ka, acc = cur
                            stt(v[:, j, :], pc, vcol(j, R_BDW + jl), acc, ALU.add, ALU.add,
                                [kc, ka, ('vecs',)], gk(gv, j))
                        cur = nxt
                        act(z[:, j, :], v[:, j, :], AF.Square, gk(gv, j), gk(gz, j))
                        if T < NT - 1:
                            copy('pool', ubuf[:, j, 0:30], ubuf[:, j, TT:TT + 30], [('u', j)], [('u', j)])
                        if j in hooks:
                            hooks[j]()

                km, mean = ('stat', 5), statf[:, 5, :]
                kq, msq = ('stat', 6), statf[:, 6, :]

                def cS(T):
                    vsq = z
                    k1, p1 = bank()
                    for c in range(NCH):
                        mm(p1, onesb, v[:, c, :], c == 0, c == 7, gk(gv, c) + [('onesb',)], [k1])
                    k2, p2 = bank()
                    for c in range(NCH):
                        mm(p2, onesb, vsq[:, c, :], c == 0, c == 7, gk(gz, c) + [('onesb',)], [k2])
                    tsc('dve', mean, p1, 1.0 / D, None, ALU.mult, None, [k1], [km])
                    tt('dve', msq, mean, mean, ALU.mult, [km], [kq])
                    stt(msq, p2, 1.0 / D, msq, ALU.mult, ALU.subtract, [k2, kq], [kq])
                    act(msq, msq, AF.Ln, [kq, ('eps',)], [kq], bias=epsl)
                    act(msq, msq, AF.Exp, [kq], [kq], scale=-0.5)
                    tt('dve', mean, mean, msq, ALU.mult, [km, kq], [km])

                def cLpre(T, c, eng='pool'):
                    kt, tmp = stat()
                    tt(eng, tmp, v[:, c, :], msq, ALU.mult, gk(gv, c) + [kq], [kt])
                    tt(eng, tmp, tmp, mean, ALU.subtract, [kt, km], [kt])
                    act(tmp, tmp, AF.Identity, [kt, ('vecs',)], [kt],
                        bias=vcol(c, R_LNB + jl), scale=vcol(c, R_LNG + jl))
                    ks, sg = stat()
                    act(sg, tmp, AF.Sigmoid, [kt], [ks])
                    return kt, tmp, ks, sg

                def cLpost(T, c, pre):
                    kt, tmp, ks, sg = pre
                    tt('dve', z[:, c, :], tmp, sg, ALU.mult, [kt, ks], gk(gz, c))

                def cO(T):
                    for j in range(NCH):
                        ko, po = bank()
                        for k in range(NCH):
                            wk_, lt = wcol(('wout', l, j // 4), k, (j % 4) * 128)
                            mm(po, lt, z[:, k, :], k == 0, k == 7, [wk_] + gk(gz, k), [ko])
                        stt(xa_(j, T), po, vcol(j, R_BOUT + jl), xa_(j, T), ALU.add, ALU.add,
                            [ko, ('x', j, T), ('vecs',)], [('x', j, T)])

                ensure_diag(0)
                cNa(0)
                cNb(0)
                emit_kv()
                for j in range(NCH):
                    cAj(0, j)
                for T in range(NT):
                    if T + 1 < NT:
                        hooks = {1: (lambda T=T: cNa(T + 1)), 3: (lambda T=T: cNb(T + 1))}
                    else:
                        hooks = {1: (lambda: aNa(0)), 3: (lambda: aNb(0)), 5: (lambda: aNa(1)),
                                 6: (lambda: [release(('win', l, q)) for q in range(2)]),
                                 7: (lambda: [release(('win', l, q)) for q in range(2, 4)])}
                    cB(T, hooks)
                    cS(T)
                    leng = 'pool' if T + 1 < NT else 'dve'
                    pres = {0: cLpre(T, 0, leng)}
                    for j in range(NCH):
                        if j + 1 < NCH:
                            pres[j + 1] = cLpre(T, j + 1, leng)
                        cLpost(T, j, pres[j])
                        if T + 1 < NT:
                            cAj(T + 1, j)
                        else:
                            aQj(0, j)
                    cO(T)
                for q in range(2):
                    release(('wout', l, q))
            else:
                sp_ = wslot(('wpool', l, 0))
                wp = wsl[:, sp_, 0:2048].rearrange("p (g k d) -> p g k d", g=4, k=2)
                P.op('pool', lambda e: e.memset(ubuf[:, :, 0:16], 0.0), (), [('u', c) for c in range(NCH)])

                def pNa(T):
                    rms_a(xs(T), xkeys(T), TT, 5)

                def pNb1(T):
                    return rms_b1(TT, 5)

                def pNb2(T, ks):
                    rms_b2(xs(T), xkeys(T), TT, R_NMIX + l, lambda c: ubuf[:, c, 16:16 + TT],
                           lambda c: [('u', c)], ks)

                def pNb(T):
                    pNb2(T, pNb1(T))

                def pW(T):
                    gp = T % 2
                    pp = g3(gp)
                    for c in range(NCH):
                        g = c // 2
                        w = 2 ** (g + 1)
                        kb, pb_ = bank()
                        for k in range(w):
                            mm(pb_, identb, ubuf[:, c, 16 - k:16 - k + TT], k == 0, k == w - 1,
                               [('u', c), ('identb',)], [kb])
                        stt(pp[:, c, :], pb_, 1.0 / w, ubuf[:, c, 16:16 + TT], ALU.mult, ALU.subtract,
                            [kb, ('u', c)], gk(gp, c))
                        if T == 0:
                            tt('dve', tmp16, pb_[:, 0:16], invc[:, g, :], ALU.mult, [kb, ('invc', g)], [('tmp16',)])
                            tt('dve', pp[:, c, 0:16], tmp16, ubuf[:, c, 16:32], ALU.subtract,
                               [('tmp16',), ('u', c)], gk(gp, c))
                        if T < NT - 1:
                            copy('pool', ubuf[:, c, 0:16], ubuf[:, c, TT:TT + 16], [('u', c)], [('u', c)])

                def pJ(T):
                    gp = T % 2
                    pp = g3(gp)
                    for j in range(NCH):
                        g = j // 2
                        ky, py = bank()
                        for kk in range(2):
                            mm(py, wp[:, g, kk, (j % 2) * 128:(j % 2) * 128 + 128], pp[:, 2 * g + kk, :],
                               kk == 0, kk == 1, [('w', sp_)] + gk(gp, 2 * g + kk), [ky])
                        stt(xa_(j, T), py, vcol(j, R_PSC + jl), xa_(j, T), ALU.mult, ALU.add,
                            [ky, ('x', j, T), ('vecs',)], [('x', j, T)])

                pNa(0)
                pNb(0)
                emit_kv()
                for T in range(NT):
                    if T + 1 < NT:
                        pNa(T + 1)
                        ks_next = pNb1(T + 1)
                    pW(T)
                    if T + 1 < NT:
                        pNb2(T + 1, ks_next)
                    else:
                        aNa(0)
                        aNb(0)
                    pJ(T)
                aNa(1)
                release(('wpool', l, 0))


            def mNa(T):
                rms_a(xs(T), xkeys(T), TT, T)

            def mNb(T):
                hh = g3(T)
                rms_b(xs(T), xkeys(T), TT, R_NMLP + l, lambda c: hh[:, c, :], lambda c: gk(T, c), T)

            def aQ(T):
                for j in range(NCH):
                    aQj(T, j)

            def aH(T):
                gq_ = AQ[T % 2]
                qT = g3(gq_)
                Es = [gen[:, ge, hd * 1024:(hd + 1) * 1024].rearrange("p (m t) -> p m t", m=2) for hd in range(4)]

                def scores(hd):
                    for mt in range(2):
                        ks_, ps_ = bank()
                        for dd in range(2):
                            mm(ps_, kT[:, 2 * hd + dd, mt * 128:(mt + 1) * 128], qT[:, 2 * hd + dd, :],
                               dd == 0, dd == 1, [('kT',)] + gk(gq_, 2 * hd + dd), [ks_])
                        act(Es[hd][:, mt, :], ps_, AF.Exp, [ks_], gk(ge, 2 * hd + mt), scale=1.0 / 16.0)

                def pv(hd):
                    E = Es[hd]
                    kz, pz = bank()
                    for mt in range(2):
                        mm(pz, onesb, E[:, mt, :], mt == 0, mt == 1, gk(ge, 2 * hd + mt) + [('onesb',)], [kz])
                    krz, rz = stat()
                    act(rz, pz, AF.Ln, [kz], [krz])
                    act(rz, rz, AF.Exp, [krz], [krz], scale=-1.0)
                    for dd in range(2):
                        ko_, po_ = bank()
                        for mt in range(2):
                            mm(po_, Vt[:, mt, (2 * hd + dd) * 128:(2 * hd + dd + 1) * 128], E[:, mt, :],
                               mt == 0, mt == 1, [('Vt',)] + gk(ge, 2 * hd + mt), [ko_])
                        tt('dve', oT[:, 2 * hd + dd, :], po_, rz, ALU.mult, [ko_, krz], gk(go, 2 * hd + dd))

                scores(0)
                for hd in range(4):
                    if hd + 1 < 4:
                        scores(hd + 1)
                    pv(hd)

            def aW(T):
                for j in range(NCH):
                    kw_, pw = bank()
                    for k in range(NCH):
                        wk_, lt = wcol(('wo', l, j // 4), k, (j % 4) * 128)
                        mm(pw, lt, oT[:, k, :], k == 0, k == 7, [wk_] + gk(go, k), [kw_])
                    tt('dve', xa_(j, T), pw, xa_(j, T), ALU.add, [kw_, ('x', j, T)], [('x', j, T)])

            if l % 2 == 1:
                aQ(0)
            aNb(1)
            for T in range(NT):
                aH(T)
                if T + 1 < NT:
                    aQ(T + 1)
                else:
                    for q in range(2):
                        release(('wq', l, q))
                if T + 2 < NT:
                    aNa(T + 2)
                if T == NT - 1:
                    mNa(0)
                    mNa(1)
                aW(T)
                if T + 2 < NT:
                    aNb(T + 2)
            for q in range(2):
                release(('wo', l, q))

            def mU(s):
                fb, T = s // NT, s % NT
                hh = g3(T)
                g1 = 4 + (s % 2)
                h1 = g3(g1)
                for i in range(NCH):
                    k1_, p1_ = bank()
                    for k in range(NCH):
                        wk_, lt = wcol(('w1', l, fb * 2 + i // 4), k, (i % 4) * 128)
                        mm(p1_, lt, hh[:, k, :], k == 0, k == 7, [wk_] + gk(T, k), [k1_])
                    if i % 2 == 0:
                        act(h1[:, i, :], p1_, AF.Relu, [k1_], gk(g1, i))
                    else:
                        tsc('dve', h1[:, i, :], p1_, 0.0, None, ALU.max, None, [k1_], gk(g1, i))
                    tt('pool', h1[:, i, :], h1[:, i, :], h1[:, i, :], ALU.mult, gk(g1, i), gk(g1, i))
                if T == NT - 1:
                    for q in range(2):
                        release(('w1', l, fb * 2 + q))

            def mD(s):
                fb, T = s // NT, s % NT
                g1 = 4 + (s % 2)
                h1 = g3(g1)
                for j in range(NCH):
                    k2_, p2_ = bank()
                    for i in range(NCH):
                        s_ = wslot(('w2', l, fb * 2 + i // 4))
                        w2u = wsl[:, s_, :].rearrange("p (k n) -> p k n", k=4)
                        mm(p2_, w2u[:, i % 4, j * 128:(j + 1) * 128], h1[:, i, :], i == 0, i == 7,
                           [('w', s_)] + gk(g1, i), [k2_])
                    tt('dve', xa_(j, T), p2_, xa_(j, T), ALU.add, [k2_, ('x', j, T)], [('x', j, T)])
                if T == NT - 1:
                    for q in range(2):
                        release(('w2', l, fb * 2 + q))

            mNb(0)
            mNa(2)
            mNb(1)
            mNa(3)
            mU(0)
            mNb(2)
            mNb(3)
            for s in range(4 * NT):
                if s + 1 < 4 * NT:
                    mU(s + 1)
                mD(s)

        def fNa(T):
            act(g3(5), xs(T), AF.Square, xkeys(T), gk(5))

        def fNb(T):
            sq = g3(5)
            bk, bp = bank()
            for c in range(NCH):
                mm(bp, onesb, sq[:, c, :], c == 0, c == 7, gk(5, c) + [('onesb',)], [bk])
            k1, s1 = stat()
            rstd_from(bp, TT, k1, s1, [bk], epsr)
            for c in range(NCH):
                stt(xa_(c, T), xa_(c, T), vcol(c, R_FIN), s1, ALU.mult, ALU.mult,
                    [('x', c, T), k1, ('vecs',)], [('x', c, T)])

        nout = [0]

        def fO(T):
            for s4 in range(4):
                t0 = T * TT + s4 * 128
                sap, skeys, dk = gstage(nout[0] % 10)
                for half in range(2):
                    bk, bp = bank()
                    for cc in range(4):
                        c = half * 4 + cc
                        P.op('pe', lambda e, c=c, cc=cc, bp=bp, t0=t0: e.transpose(
                            bp[:, cc * 128:(cc + 1) * 128], xT[:, c, t0:t0 + 128], identf),
                            [('x', c, T), ('identf',)], [bk])
                    copy('act' if half == 0 else 'dve', sap[:, half * 512:(half + 1) * 512], bp,
                         [bk], skeys[half * 2:half * 2 + 2])
                P.dma('sp', [lambda e, sap=sap, t0=t0: e.dma_start(out=out_d[t0:t0 + 128, :], in_=sap)],
                      dk, skeys, [('out', nout[0])])
                nout[0] += 1

        if final:
            fNa(0)
            fNb(0)
            for T in range(NT):
                if T + 1 < NT:
                    fNa(T + 1)
                fO(T)
                if T + 1 < NT:
                    fNb(T + 1)
        else:
            for T in range(NT):
                fO(T)
        nout = nout[0]
        P.wait('sp', [('out', i) for i in range(nout)])
        P.emit(nc, stack)
    return nc


LAUNCH_GROUPS = [(0, 4)]
_CACHE = {}


def _prog(l0, l1, final):
    key = (l0, l1, final)
    if key not in _CACHE:
        _CACHE[key] = build(l0, l1, final)
    return _CACHE[key]


def kernel(**inp):
    f = lambda a: np.ascontiguousarray(np.asarray(a, dtype=np.float32))
    B = 8
    x = f(inp["x"])
    mem = f(inp["mem"])
    shared = {
        "norm_mix": f(inp["norm_mix"]), "norm_xattn": f(inp["norm_xattn"]), "norm_mlp": f(inp["norm_mlp"]),
        "conv_b_in": f(inp["conv_b_in"]).reshape(4, D), "conv_b_dw": f(inp["conv_b_dw"]),
        "conv_ln_g": f(inp["conv_ln_g"]), "conv_ln_b": f(inp["conv_ln_b"]), "conv_b_out": f(inp["conv_b_out"]),
        "pool_scale": f(inp["pool_scale"]), "mem_norm": f(inp["mem_norm"]).reshape(1, D),
        "final_norm": f(inp["final_norm"]).reshape(1, D), "conv_w_dw": f(inp["conv_w_dw"]).reshape(2 * CW, D),
        "conv_w_in": f(inp["conv_w_in"]), "conv_w_out": f(inp["conv_w_out"]), "pool_w": f(inp["pool_w"]),
        "xattn_wq": f(inp["xattn_wq"]), "xattn_wk": f(inp["xattn_wk"]), "xattn_wv": f(inp["xattn_wv"]),
        "xattn_wo": f(inp["xattn_wo"]), "mlp_w1": f(inp["mlp_w1"]), "mlp_w2": f(inp["mlp_w2"]),
    }
    cur = [x[b] for b in range(B)]
    for (l0, l1) in LAUNCH_GROUPS:
        nc = _prog(l0, l1, l1 == DEPTH)
        in_maps = [dict(shared, x=cur[b], mem=mem[b]) for b in range(B)]
        res = run_bass_kernel_spmd(nc, in_maps, core_ids=list(range(B)))
        cur = [np.asarray(res.results[b]["out"], dtype=np.float32) for b in range(B)]
    return np.stack(cur, axis=0)
```

```python
import contextlib
import numpy as np
import concourse.bass as bass
import concourse.mybir as mybir
from concourse.bass_utils import run_bass_kernel_spmd

F32 = mybir.dt.float32
BF16 = mybir.dt.bfloat16
I32 = mybir.dt.int32
ALU = mybir.AluOpType
AF = mybir.ActivationFunctionType

D = 1024
S = 2048
MEM = 256
DEPTH = 4
NCH = 8
TT = 512
NT = S // TT
CW = 31
NDV = 0
NSLOT = 6
NSTG = 1
NGEN = 6
NSTAT = 7
RMS_EPS = 1e-6
LN_EPS = 1e-5

R_NMIX, R_NXA, R_NMLP, R_BIN, R_BDW, R_LNG, R_LNB, R_BOUT, R_PSC, R_MEMN, R_FIN, R_WDW = (
    0, 4, 8, 12, 16, 18, 20, 22, 24, 26, 27, 28)
NV = 28 + 2 * CW


class Prog:
    def __init__(self):
        self.ins = []

    def op(self, eng, fn, reads=(), writes=()):
        self.ins.append(dict(eng=eng, fn=fn, reads=tuple(reads), writes=tuple(writes), dma=None))

    def dma(self, eng, fns, key, reads=(), writes=()):
        self.ins.append(dict(eng=eng, fn=list(fns), reads=tuple(reads), writes=tuple(writes), dma=key))

    def wait(self, eng, reads):
        self.ins.append(dict(eng=eng, fn=None, reads=tuple(reads), writes=(), dma=None))

    def analyze(self):
        lastw = {}
        rdrs = {}
        ins = self.ins
        for i, I in enumerate(ins):
            deps = {}
            myek = ('dma', I['dma']) if I['dma'] else I['eng']

            def add(j, kind, I=I, deps=deps):
                J = ins[j]
                ek = ('dma', J['dma']) if J['dma'] else J['eng']
                if (not J['dma']) and (not I['dma']) and J['eng'] == I['eng']:
                    if I['eng'] == 'pe':
                        return
                if deps.get(ek, -1) < j:
                    deps[ek] = j

            for r in I['reads']:
                if r in lastw:
                    add(lastw[r], 'raw')
            for w in I['writes']:
                if w in lastw:
                    add(lastw[w], 'waw')
                for ek, j in rdrs.get(w, {}).items():
                    if j != i:
                        add(j, 'war')
            I['deps'] = deps
            for r in I['reads']:
                rdrs.setdefault(r, {})[myek] = i
            for w in I['writes']:
                lastw[w] = i
                rdrs[w] = {}
        for I in ins:
            I['sig'] = False
        for I in ins:
            for ek, j in I['deps'].items():
                ins[j]['sig'] = True
        cnt = {}
        for I in ins:
            if I['dma']:
                k = ('dma', I['dma'])
                cnt[k] = cnt.get(k, 0) + 16 * len(I['fn'])
                I['cnt'] = cnt[k]
            elif I['sig']:
                k = I['eng']
                cnt[k] = cnt.get(k, 0) + 1
                I['cnt'] = cnt[k]
        self.dma_keys = sorted({I['dma'] for I in ins if I['dma']}, key=str)

    def emit(self, nc, stack):
        self.analyze()
        ins = self.ins
        sems = {}
        for e in ('pe', 'act', 'dve', 'pool', 'sp'):
            sems[e] = stack.enter_context(nc.semaphore("s_" + e))
        for k in self.dma_keys:
            sems[('dma', k)] = stack.enter_context(nc.semaphore("d_" + "_".join(str(z) for z in k)))
        block = stack.enter_context(nc.Block())

        def run(ename):
            def body(eng):
                known = {}
                for I in ins:
                    if I['eng'] != ename:
                        continue
                    for ek, j in I['deps'].items():
                        need = ins[j]['cnt']
                        if known.get(ek, 0) < need:
                            eng.wait_ge(sems[ek], need)
                            known[ek] = need
                    if I['fn'] is None:
                        continue
                    if I['dma']:
                        for f in I['fn']:
                            f(eng).then_inc(sems[('dma', I['dma'])], 16)
                    else:
                        r = I['fn'](eng)
                        if I['sig']:
                            r.then_inc(sems[ename], 1)
            return body

        block.tensor(run('pe'))
        block.scalar(run('act'))
        block.vector(run('dve'))
        block.gpsimd(run('pool'))
        block.sync(run('sp'))


def build(l0, l1, final):
    nc = bass.Bass("TRN2", target_bir_lowering=False)
    dt = lambda n, sh, kind="ExternalInput": nc.dram_tensor(n, sh, F32, kind=kind).ap()
    x_d = dt("x", [S, D])
    mem_d = dt("mem", [MEM, D])
    vec_d = {
        "norm_mix": (dt("norm_mix", [4, D]), R_NMIX), "norm_xattn": (dt("norm_xattn", [4, D]), R_NXA),
        "norm_mlp": (dt("norm_mlp", [4, D]), R_NMLP), "conv_b_in": (dt("conv_b_in", [4, D]), R_BIN),
        "conv_b_dw": (dt("conv_b_dw", [2, D]), R_BDW), "conv_ln_g": (dt("conv_ln_g", [2, D]), R_LNG),
        "conv_ln_b": (dt("conv_ln_b", [2, D]), R_LNB), "conv_b_out": (dt("conv_b_out", [2, D]), R_BOUT),
        "pool_scale": (dt("pool_scale", [2, D]), R_PSC), "mem_norm": (dt("mem_norm", [1, D]), R_MEMN),
        "final_norm": (dt("final_norm", [1, D]), R_FIN), "conv_w_dw": (dt("conv_w_dw", [2 * CW, D]), R_WDW),
    }
    w_in_d = dt("conv_w_in", [2, D, 2 * D])
    w_out_d = dt("conv_w_out", [2, D, D])
    pool_w_d = dt("pool_w", [2, 4, 256, 256])
    wq_d = dt("xattn_wq", [4, D, D])
    wk_d = dt("xattn_wk", [4, D, D])
    wv_d = dt("xattn_wv", [4, D, D])
    wo_d = dt("xattn_wo", [4, D, D])
    w1_d = dt("mlp_w1", [4, D, 4 * D])
    w2_d = dt("mlp_w2", [4, 4 * D, D])
    out_d = dt("out", [S, D], kind="ExternalOutput")

    stack = contextlib.ExitStack()
    with stack:
        sb = lambda n, sh, d: stack.enter_context(nc.sbuf_tensor(n, sh, d))[:]
        xT = sb("xT", [128, NCH, S], F32)
        wsl = sb("wsl", [128, NSLOT, 4096], BF16)
        stg = sb("stg", [128, NSTG, 1024], F32)
        gen = sb("gen", [128, NGEN, 4096], BF16)
        ubuf = sb("ubuf", [128, NCH, 544], BF16)
        statf = sb("statf", [128, NSTAT, TT], F32)
        kT = sb("kT", [128, NCH, MEM], BF16)
        Vt = sb("Vt", [128, 2, D], BF16)
        memnT = sb("memnT", [128, NCH, MEM], BF16)
        vecs = sb("vecs", [128, NCH, 128], F32)
        identf = sb("identf", [128, 128], F32)
        identb = sb("identb", [128, 128], BF16)
        onesb = sb("onesb", [128, 128], BF16)
        invc = sb("invc", [128, 4, 16], F32)
        cnt16 = sb("cnt16", [128, 16], F32)
        tmp16 = sb("tmp16", [128, 16], F32)
        epst = sb("epst", [128, 2], F32)
        epsr = epst[:, 0:1]
        epsl = epst[:, 1:2]
        psum = stack.enter_context(nc.psum_tensor("ps", [128, 8, TT], F32))[:]

        P = Prog()
        st = dict(pb=0, stg=0, gen=0, stat=0)

        def bank():
            b = st['pb'] % 8
            st['pb'] += 1
            return ('ps', b), psum[:, b, :]

        def stage():
            s = st['stg'] % NSTG
            st['stg'] += 1
            return s

        def gk(g, i=None):
            if i is None:
                return [('gen', g, q) for q in range(8)]
            return [('gen', g, i)]

        def stat():
            s = st['stat'] % 5
            st['stat'] += 1
            return ('stat', s), statf[:, s, :]

        def g3(g, n=TT):
            return gen[:, g, 0:8 * n].rearrange("p (c t) -> p c t", c=8)

        def vcol(c, v):
            return vecs[:, c, v:v + 1]

        def mm(out, lhsT, rhs, start, stop, reads, writes):
            P.op('pe', lambda e: e.matmul(out, lhsT, rhs, start=start, stop=stop), reads, writes)

        def act(out, in_, func, reads, writes, bias=None, scale=None):
            kw = {}
            if bias is not None:
                kw['bias'] = bias
            if scale is not None:
                kw['scale'] = scale
            P.op('act', lambda e: e.activation(out, in_, func, **kw), reads, writes)

        def tsc(eng, out, in0, s1, s2, op0, op1, reads, writes):
            if op1 is None:
                P.op(eng, lambda e: e.tensor_scalar(out, in0, s1, None, op0), reads, writes)
            else:
                P.op(eng, lambda e: e.tensor_scalar(out, in0, s1, s2, op0, op1), reads, writes)

        def stt(out, in0, scalar, in1, op0, op1, reads, writes):
            P.op('dve', lambda e: e.scalar_tensor_tensor(out, in0, scalar, in1, op0, op1), reads, writes)

        def tt(eng, out, in0, in1, op, reads, writes):
            P.op(eng, lambda e: e.tensor_tensor(out, in0, in1, op), reads, writes)

        def copy(eng, out, in_, reads, writes):
            if eng == 'act':
                P.op('act', lambda e: e.copy(out, in_), reads, writes)
            else:
                P.op(eng, lambda e: e.tensor_copy(out, in_), reads, writes)

        P.op('pool', lambda e: e.memset(identf, 0.0), (), [('identf',)])
        P.op('pool', lambda e: e.affine_select(out=identf, in_=identf, pattern=[[-1, 128]],
                                                compare_op=ALU.not_equal, fill=1.0, base=0,
                                                channel_multiplier=1), [('identf',)], [('identf',)])
        copy('pool', identb, identf, [('identf',)], [('identb',)])
        P.op('pool', lambda e: e.memset(onesb, 1.0), (), [('onesb',)])
        P.op('pool', lambda e: e.memset(epsr, RMS_EPS), (), [('eps',)])
        P.op('pool', lambda e: e.memset(epsl, LN_EPS), (), [('eps',)])
        P.op('pool', lambda e: e.iota(cnt16, pattern=[[1, 16]], base=1, channel_multiplier=0,
                                      allow_small_or_imprecise_dtypes=True), (), [('cnt16',)])
        for g in range(4):
            tsc('pool', invc[:, g, :], cnt16, float(2 ** (g + 1)), None, ALU.min, None,
                [('cnt16',)], [('invc', g)])
            P.op('dve', lambda e, g=g: e.reciprocal(invc[:, g, :], invc[:, g, :]), [('invc', g)], [('invc', g)])

        s0 = stage()
        fns = []
        for name, (ap, r0) in vec_d.items():
            nr = ap.shape[0]
            fns.append(lambda e, ap=ap, r0=r0, nr=nr: e.dma_start(out=stg[r0:r0 + nr, s0, :], in_=ap))
        P.dma('sp', fns, ('stg', s0), (), [('stg', s0)])
        for half in range(2):
            bk, bp = bank()
            for cc in range(4):
                c = half * 4 + cc
                P.op('pe', lambda e, c=c, cc=cc, bp=bp: e.transpose(
                    bp[:, cc * 128:cc * 128 + NV], stg[0:NV, s0, c * 128:(c + 1) * 128], identf[0:NV, 0:NV]),
                    [('stg', s0), ('identf',)], [bk])
            copy('dve', vecs[:, half * 4:half * 4 + 4, 0:NV],
                 bp.rearrange("p (c v) -> p c v", v=128)[:, :, 0:NV], [bk], [('vecs',)])

        units = []
        for l in range(l0, l1):
            jl = l // 2
            for nm, wd in (('wk', wk_d), ('wv', wv_d)):
                for q in range(2):
                    units.append(((nm, l, q), 'col', wd[l], q * 512, 4))
            if l % 2 == 0:
                for q in range(4):
                    units.append((('win', l, q), 'col', w_in_d[jl], q * 512, 4))
                for q in range(2):
                    units.append((('wout', l, q), 'col', w_out_d[jl], q * 512, 4))
            else:
                units.append((('wpool', l, 0), 'pool', pool_w_d[jl], 0, 2))
            for nm, wd in (('wq', wq_d), ('wo', wo_d)):
                for q in range(2):
                    units.append(((nm, l, q), 'col', wd[l], q * 512, 4))
            for fb in range(4):
                for q in range(2):
                    units.append((('w1', l, fb * 2 + q), 'col', w1_d[l], fb * 1024 + q * 512, 4))
                for q in range(2):
                    units.append((('w2', l, fb * 2 + q), 'row', w2_d[l], fb * 1024 + q * 512, 4))
        wm = dict(next=0, slot_of={}, free=list(range(NSLOT)))

        def load_unit(idx, slot):
            uid, kind, W, a, npc = units[idx]
            wm['slot_of'][uid] = slot
            fns = []
            for p in range(npc):
                dstp = wsl[:, slot, p * 1024:(p + 1) * 1024]
                if kind == 'col':
                    src = W[p * 256:(p + 1) * 256, a:a + 512].rearrange("(k p) n -> p k n", p=128)
                    dst = dstp.rearrange("p (k n) -> p k n", k=2)
                elif kind == 'row':
                    src = W[a + p * 128:a + (p + 1) * 128, :]
                    dst = dstp
                else:
                    src = W[2 * p:2 * p + 2].rearrange("g (kk p) d -> p (g kk) d", p=128)
                    dst = dstp.rearrange("p (k n) -> p k n", k=4)
                fns.append(lambda e, src=src, dst=dst: e.dma_start(out=dst, in_=src))
            P.dma('pool', fns, ('wd', slot), (), [('w', slot)])

        def pump():
            while wm['free'] and wm['next'] < len(units):
                load_unit(wm['next'], wm['free'].pop(0))
                wm['next'] += 1

        def wslot(uid):
            assert uid in wm['slot_of'], uid
            return wm['slot_of'][uid]

        def release(uid):
            wm['free'].append(wm['slot_of'].pop(uid))
            pump()

        def wcol(uid, k, off):
            s = wslot(uid)
            return ('w', s), wsl[:, s, :].rearrange("p (k n) -> p k n", k=8)[:, k, off:off + 128]

        pump()

        def gstage(i):
            g, hf = i // 2, i % 2
            return (gen[:, g, hf * 2048:(hf + 1) * 2048].bitcast(F32),
                    [('gen', g, q) for q in range(hf * 4, hf * 4 + 4)], ('gs', i))

        def load_rows(src_d, nrows, dst3, dkeys, bufs):
            for tt_ in range(nrows // 128):
                sap, skeys, dk = bufs[tt_ % len(bufs)]
                P.dma('sp', [lambda e, sap=sap, tt_=tt_: e.dma_start(out=sap,
                                                                     in_=src_d[tt_ * 128:(tt_ + 1) * 128, :])],
                      dk, (), skeys)
                for half in range(2):
                    bk, bp = bank()
                    for cc in range(4):
                        c = half * 4 + cc
                        P.op('pe', lambda e, c=c, cc=cc, bp=bp, sap=sap: e.transpose(
                            bp[:, cc * 128:(cc + 1) * 128], sap[:, c * 128:(c + 1) * 128], identf),
                            skeys + [('identf',)], [bk])
                    copy('act' if half == 0 else 'dve',
                         dst3(half * 4, half * 4 + 4, tt_ * 128, (tt_ + 1) * 128),
                         bp.rearrange("p (c v) -> p c v", v=128), [bk], dkeys(tt_, half))

        def rstd_from(bp, n, k1, s1, bkeys, epsap):
            act(s1[:, 0:n], bp[:, 0:n], AF.Ln, bkeys + [('eps',)], [k1], bias=epsap, scale=1.0 / D)
            act(s1[:, 0:n], s1[:, 0:n], AF.Exp, [k1], [k1], scale=-0.5)

        def rms_a(src3, srckeys, n, gs):
            act(g3(gs, n), src3, AF.Square, srckeys, gk(gs))

        def rms_b1(n, gs):
            sq = g3(gs, n)
            bk, bp = bank()
            for c in range(NCH):
                mm(bp[:, 0:n], onesb, sq[:, c, :], c == 0, c == NCH - 1, gk(gs, c) + [('onesb',)], [bk])
            k1, s1 = stat()
            rstd_from(bp, n, k1, s1, [bk], epsr)
            return k1, s1

        def rms_b2(src3, srckeys, n, grow, dst_c, dstkeys, ks):
            k1, s1 = ks
            for c in range(NCH):
                stt(dst_c(c), src3[:, c, :], vcol(c, grow), s1[:, 0:n], ALU.mult, ALU.mult,
                    [srckeys[c], k1, ('vecs',)], dstkeys(c))

        def rms_b(src3, srckeys, n, grow, dst_c, dstkeys, gs):
            rms_b2(src3, srckeys, n, grow, dst_c, dstkeys, rms_b1(n, gs))

        def rmsnorm(src3, srckeys, n, grow, dst_c, dstkeys, gs):
            rms_a(src3, srckeys, n, gs)
            rms_b(src3, srckeys, n, grow, dst_c, dstkeys, gs)

        xkeys = lambda T: [('x', c, T) for c in range(NCH)]
        xs = lambda T: xT[:, :, T * TT:(T + 1) * TT]
        xa_ = lambda j, T: xT[:, j, T * TT:(T + 1) * TT]

        load_rows(x_d, S, lambda c0, c1, t0, t1: xT[:, c0:c1, t0:t1],
                  lambda tt_, half: [('x', c, tt_ // 4) for c in range(half * 4, half * 4 + 4)],
                  [gstage(i) for i in range(2 * NGEN)])

        memT = gen[:, 0, :].bitcast(F32).rearrange("p (c t) -> p c t", c=8)
        load_rows(mem_d, MEM, lambda c0, c1, t0, t1: memT[:, c0:c1, t0:t1], lambda tt_, half: gk(0),
                  [gstage(i) for i in (2, 3)])
        rmsnorm(memT, [('gen', 0, c) for c in range(8)], MEM, R_MEMN, lambda c: memnT[:, c, :],
                lambda c: [('memnT',)], 5)

        for l in range(l0, l1):
            jl = l // 2
            AH = (0, 5)
            AQ = (1, 4)
            go, ge = 2, 3
            oT = g3(go)

            def aNa(T):
                rms_a(xs(T), xkeys(T), TT, AH[T % 2])

            def aNb(T):
                gh = AH[T % 2]
                h = g3(gh)
                rms_b(xs(T), xkeys(T), TT, R_NXA + l, lambda c: h[:, c, :], lambda c: gk(gh, c), gh)

            def aQj(T, j):
                gh = AH[T % 2]
                h = g3(gh)
                gq_ = AQ[T % 2]
                qT = g3(gq_)
                kq_, pq = bank()
                for k in range(NCH):
                    wk_, lt = wcol(('wq', l, j // 4), k, (j % 4) * 128)
                    mm(pq, lt, h[:, k, :], k == 0, k == 7, [wk_] + gk(gh, k), [kq_])
                copy('act' if j % 2 == 0 else 'dve', qT[:, j, :], pq, [kq_], gk(gq_, j))

            def emit_kv():
                for j in range(NCH):
                    kk_, pk = bank()
                    for k in range(NCH):
                        wk_, lt = wcol(('wk', l, j // 4), k, (j % 4) * 128)
                        mm(pk[:, 0:MEM], lt, memnT[:, k, :], k == 0, k == 7, [wk_, ('memnT',)], [kk_])
                    copy('act', kT[:, j, :], pk[:, 0:MEM], [kk_], [('kT',)])
                for mt in range(2):
                    for q in range(2):
                        s_ = wslot(('wv', l, q))
                        wvu = wsl[:, s_, :].rearrange("p (k n) -> p k n", k=8)
                        kv_, pv = bank()
                        for k in range(NCH):
                            mm(pv, memnT[:, k, mt * 128:(mt + 1) * 128], wvu[:, k, :], k == 0, k == 7,
                               [('w', s_), ('memnT',)], [kv_])
                        copy('dve', Vt[:, mt, q * 512:(q + 1) * 512], pv, [kv_], [('Vt',)])
                for q in range(2):
                    release(('wk', l, q))
                for q in range(2):
                    release(('wv', l, q))

            if l % 2 == 0:
                rw = R_WDW + jl * CW
                HT = (0, 5)
                gv, gz = 1, 4
                v = g3(gv)
                z = g3(gz)
                P.op('pool', lambda e: e.memset(ubuf[:, :, 0:30], 0.0), (), [('u', c) for c in range(NCH)])

                def cNa(T):
                    rms_a(xs(T), xkeys(T), TT, HT[T % 2])

                def cNb(T):
                    gh = HT[T % 2]
                    h = g3(gh)
                    rms_b(xs(T), xkeys(T), TT, R_NMIX + l, lambda c: h[:, c, :], lambda c: gk(gh, c), gh)

                def cAj(T, j):
                    gh = HT[T % 2]
                    h = g3(gh)
                    ka, pa = bank()
                    for k in range(NCH):
                        wk_, lt = wcol(('win', l, j // 4), k, (j % 4) * 128)
                        mm(pa, lt, h[:, k, :], k == 0, k == 7, [wk_] + gk(gh, k), [ka])
                    kg, pg = bank()
                    for k in range(NCH):
                        wk_, lt = wcol(('win', l, 2 + j // 4), k, (j % 4) * 128)
                        mm(pg, lt, h[:, k, :], k == 0, k == 7, [wk_] + gk(gh, k), [kg])
                    ks, sg = stat()
                    act(sg, pg, AF.Sigmoid, [kg, ('vecs',)], [ks], bias=vcol(j, R_BIN + 2 * jl + 1))
                    stt(ubuf[:, j, 30:30 + TT], pa, vcol(j, R_BIN + 2 * jl), sg, ALU.add, ALU.mult,
                        [ka, ks, ('vecs',)], [('u', j)])

                dgs = [gen[:, 2 + q, 0:CW * 128].rearrange("p (k m) -> p k m", k=CW) for q in range(2)]
                dbuilt = set()

                def ensure_diag(n):
                    if n in dbuilt or n >= NT * NCH:
                        return
                    dbuilt.add(n)
                    j, gd = n % NCH, 2 + (n % 2)
                    win = vecs[:, j, rw + NDV:rw + CW].unsqueeze(2).broadcast_to([128, CW - NDV, 128])
                    extra = [('x', 7, n // NCH - 1)] if (n % NCH == 1 and n >= NCH) else []
                    P.op('pool', lambda e, dg=dgs[n % 2], win=win: e.affine_select(
                        out=dg[:, NDV:CW, :], in_=win, pattern=[[0, CW - NDV], [-1, 128]],
                        compare_op=ALU.is_equal, fill=0.0, base=0, channel_multiplier=1),
                        [('vecs',)] + extra, gk(gd))

                def taps(T, j):
                    if NDV == 0:
                        return None
                    ka, acc = stat()
                    tsc('dve', acc, ubuf[:, j, 0:TT], vcol(j, rw), None, ALU.mult, None,
                        [('u', j), ('vecs',)], [ka])
                    for k in range(1, NDV):
                        stt(acc, ubuf[:, j, k:k + TT], vcol(j, rw + k), acc, ALU.mult, ALU.add,
                            [('u', j), ka, ('vecs',)], [ka])
                    return ka, acc

                def cB(T, hooks):
                    cur = taps(T, 0)
                    for j in range(NCH):
                        n = T * NCH + j
                        gd = 2 + (n % 2)
                        dg = dgs[n % 2]
                        ensure_diag(n)
                        ensure_diag(n + 1)
                        nxt = taps(T, j + 1) if j + 1 < NCH else None
                        kc, pc = bank()
                        for k in range(NDV, CW):
                            mm(pc, dg[:, k, :], ubuf[:, j, k:k + TT], k == NDV, k == CW - 1,
                               gk(gd) + [('u', j)], [kc])
                        if cur is None:
                            act(v[:, j, :], pc, AF.Identity, [kc, ('vecs',)], gk(gv, j),
                                bias=vcol(j, R_BDW + jl))
                        else:
                            ka, acc = cur
                            stt(v[:, j, :], pc, vcol(j, R_BDW + jl), acc, ALU.add, ALU.add,
                                [kc, ka, ('vecs',)], gk(gv, j))
                        cur = nxt
                        act(z[:, j, :], v[:, j, :], AF.Square, gk(gv, j), gk(gz, j))
                        if T < NT - 1:
                            copy('pool', ubuf[:, j, 0:30], ubuf[:, j, TT:TT + 30], [('u', j)], [('u', j)])
                        if j in hooks:
                            hooks[j]()

                km, mean = ('stat', 5), statf[:, 5, :]
                kq, msq = ('stat', 6), statf[:, 6, :]

                def cS(T):
                    vsq = z
                    k1, p1 = bank()
                    for c in range(NCH):
                        mm(p1, onesb, v[:, c, :], c == 0, c == 7, gk(gv, c) + [('onesb',)], [k1])
                    k2, p2 = bank()
                    for c in range(NCH):
                        mm(p2, onesb, vsq[:, c, :], c == 0, c == 7, gk(gz, c) + [('onesb',)], [k2])
                    tsc('dve', mean, p1, 1.0 / D, None, ALU.mult, None, [k1], [km])
                    tt('dve', msq, mean, mean, ALU.mult, [km], [kq])
                    stt(msq, p2, 1.0 / D, msq, ALU.mult, ALU.subtract, [k2, kq], [kq])
                    act(msq, msq, AF.Ln, [kq, ('eps',)], [kq], bias=epsl)
                    act(msq, msq, AF.Exp, [kq], [kq], scale=-0.5)
                    tt('dve', mean, mean, msq, ALU.mult, [km, kq], [km])

                def cLpre(T, c, eng='pool'):
                    kt, tmp = stat()
                    tt(eng, tmp, v[:, c, :], msq, ALU.mult, gk(gv, c) + [kq], [kt])
                    tt(eng, tmp, tmp, mean, ALU.subtract, [kt, km], [kt])
                    act(tmp, tmp, AF.Identity, [kt, ('vecs',)], [kt],
                        bias=vcol(c, R_LNB + jl), scale=vcol(c, R_LNG + jl))
                    ks, sg = stat()
                    act(sg, tmp, AF.Sigmoid, [kt], [ks])
                    return kt, tmp, ks, sg

                def cLpost(T, c, pre):
                    kt, tmp, ks, sg = pre
                    tt('dve', z[:, c, :], tmp, sg, ALU.mult, [kt, ks], gk(gz, c))

                def cO(T):
                    for j in range(NCH):
                        ko, po = bank()
                        for k in range(NCH):
                            wk_, lt = wcol(('wout', l, j // 4), k, (j % 4) * 128)
                            mm(po, lt, z[:, k, :], k == 0, k == 7, [wk_] + gk(gz, k), [ko])
                        stt(xa_(j, T), po, vcol(j, R_BOUT + jl), xa_(j, T), ALU.add, ALU.add,
                            [ko, ('x', j, T), ('vecs',)], [('x', j, T)])

                ensure_diag(0)
                cNa(0)
                cNb(0)
                emit_kv()
                for j in range(NCH):
                    cAj(0, j)
                for T in range(NT):
                    if T + 1 < NT:
                        hooks = {1: (lambda T=T: cNa(T + 1)), 3: (lambda T=T: cNb(T + 1))}
                    else:
                        hooks = {1: (lambda: aNa(0)), 3: (lambda: aNb(0)), 5: (lambda: aNa(1)),
                                 6: (lambda: [release(('win', l, q)) for q in range(2)]),
                                 7: (lambda: [release(('win', l, q)) for q in range(2, 4)])}
                    cB(T, hooks)
                    cS(T)
                    leng = 'pool' if T + 1 < NT else 'dve'
                    pres = {0: cLpre(T, 0, leng)}
                    for j in range(NCH):
                        if j + 1 < NCH:
                            pres[j + 1] = cLpre(T, j + 1, leng)
                        cLpost(T, j, pres[j])
                        if T + 1 < NT:
                            cAj(T + 1, j)
                        else:
                            aQj(0, j)
                    cO(T)
                for q in range(2):
                    release(('wout', l, q))
            else:
                sp_ = wslot(('wpool', l, 0))
                wp = wsl[:, sp_, 0:2048].rearrange("p (g k d) -> p g k d", g=4, k=2)
                P.op('pool', lambda e: e.memset(ubuf[:, :, 0:16], 0.0), (), [('u', c) for c in range(NCH)])

                def pNa(T):
                    rms_a(xs(T), xkeys(T), TT, 5)

                def pNb1(T):
                    return rms_b1(TT, 5)

                def pNb2c(T, ks, c):
                    k1, s1 = ks
                    stt(ubuf[:, c, 16:16 + TT], xT[:, c, T * TT:(T + 1) * TT], vcol(c, R_NMIX + l), s1,
                        ALU.mult, ALU.mult, [('x', c, T), k1, ('vecs',)], [('u', c)])

                def pNb2(T, ks):
                    for c in range(NCH):
                        pNb2c(T, ks, c)

                def pNb(T):
                    pNb2(T, pNb1(T))

                def pW(T):
                    gp = T % 2
                    pp = g3(gp)
                    for c in range(NCH):
                        g = c // 2
                        w = 2 ** (g + 1)
                        kb, pb_ = bank()
                        for k in range(w):
                            mm(pb_, identb, ubuf[:, c, 16 - k:16 - k + TT], k == 0, k == w - 1,
                               [('u', c), ('identb',)], [kb])
                        stt(pp[:, c, :], pb_, 1.0 / w, ubuf[:, c, 16:16 + TT], ALU.mult, ALU.subtract,
                            [kb, ('u', c)], gk(gp, c))
                        if T == 0:
                            tt('dve', tmp16, pb_[:, 0:16], invc[:, g, :], ALU.mult, [kb, ('invc', g)], [('tmp16',)])
                            tt('dve', pp[:, c, 0:16], tmp16, ubuf[:, c, 16:32], ALU.subtract,
                               [('tmp16',), ('u', c)], gk(gp, c))
                        if T < NT - 1:
                            copy('pool', ubuf[:, c, 0:16], ubuf[:, c, TT:TT + 16], [('u', c)], [('u', c)])

                def pJj(T, j):
                    gp = T % 2
                    pp = g3(gp)
                    g = j // 2
                    ky, py = bank()
                    for kk in range(2):
                        mm(py, wp[:, g, kk, (j % 2) * 128:(j % 2) * 128 + 128], pp[:, 2 * g + kk, :],
                           kk == 0, kk == 1, [('w', sp_)] + gk(gp, 2 * g + kk), [ky])
                    stt(xa_(j, T), py, vcol(j, R_PSC + jl), xa_(j, T), ALU.mult, ALU.add,
                        [ky, ('x', j, T), ('vecs',)], [('x', j, T)])

                def pJ(T):
                    for j in range(NCH):
                        pJj(T, j)

                pNa(0)
                pNb(0)
                emit_kv()
                for T in range(NT):
                    if T + 1 < NT:
                        pNa(T + 1)
                        ks_next = pNb1(T + 1)
                    pW(T)
                    if T + 1 < NT:
                        for i in range(NCH):
                            pNb2c(T + 1, ks_next, i)
                            pJj(T, i)
                    else:
                        aNa(0)
                        aNb(0)
                        pJ(T)
                aNa(1)
                release(('wpool', l, 0))


            def mNa(T):
                rms_a(xs(T), xkeys(T), TT, T)

            def mNb(T):
                hh = g3(T)
                rms_b(xs(T), xkeys(T), TT, R_NMLP + l, lambda c: hh[:, c, :], lambda c: gk(T, c), T)

            def aQ(T):
                for j in range(NCH):
                    aQj(T, j)

            def aH(T):
                gq_ = AQ[T % 2]
                qT = g3(gq_)
                Es = [gen[:, ge, hd * 1024:(hd + 1) * 1024].rearrange("p (m t) -> p m t", m=2) for hd in range(4)]

                def scores(hd):
                    for mt in range(2):
                        ks_, ps_ = bank()
                        for dd in range(2):
                            mm(ps_, kT[:, 2 * hd + dd, mt * 128:(mt + 1) * 128], qT[:, 2 * hd + dd, :],
                               dd == 0, dd == 1, [('kT',)] + gk(gq_, 2 * hd + dd), [ks_])
                        act(Es[hd][:, mt, :], ps_, AF.Exp, [ks_], gk(ge, 2 * hd + mt), scale=1.0 / 16.0)

                def pv(hd):
                    E = Es[hd]
                    kz, pz = bank()
                    for mt in range(2):
                        mm(pz, onesb, E[:, mt, :], mt == 0, mt == 1, gk(ge, 2 * hd + mt) + [('onesb',)], [kz])
                    krz, rz = stat()
                    act(rz, pz, AF.Ln, [kz], [krz])
                    act(rz, rz, AF.Exp, [krz], [krz], scale=-1.0)
                    for dd in range(2):
                        ko_, po_ = bank()
                        for mt in range(2):
                            mm(po_, Vt[:, mt, (2 * hd + dd) * 128:(2 * hd + dd + 1) * 128], E[:, mt, :],
                               mt == 0, mt == 1, [('Vt',)] + gk(ge, 2 * hd + mt), [ko_])
                        tt('dve', oT[:, 2 * hd + dd, :], po_, rz, ALU.mult, [ko_, krz], gk(go, 2 * hd + dd))

                scores(0)
                for hd in range(4):
                    if hd + 1 < 4:
                        scores(hd + 1)
                    pv(hd)

            def aW(T):
                for j in range(NCH):
                    kw_, pw = bank()
                    for k in range(NCH):
                        wk_, lt = wcol(('wo', l, j // 4), k, (j % 4) * 128)
                        mm(pw, lt, oT[:, k, :], k == 0, k == 7, [wk_] + gk(go, k), [kw_])
                    tt('dve', xa_(j, T), pw, xa_(j, T), ALU.add, [kw_, ('x', j, T)], [('x', j, T)])

            if l % 2 == 1:
                aQ(0)
            aNb(1)
            for T in range(NT):
                aH(T)
                if T + 1 < NT:
                    aQ(T + 1)
                else:
                    for q in range(2):
                        release(('wq', l, q))
                if T + 2 < NT:
                    aNa(T + 2)
                if T == NT - 1:
                    mNa(0)
                    mNa(1)
                aW(T)
                if T + 2 < NT:
                    aNb(T + 2)
            for q in range(2):
                release(('wo', l, q))

            def mU(s):
                fb, T = s // NT, s % NT
                hh = g3(T)
                g1 = 4 + (s % 2)
                h1 = g3(g1)
                for i in range(NCH):
                    k1_, p1_ = bank()
                    for k in range(NCH):
                        wk_, lt = wcol(('w1', l, fb * 2 + i // 4), k, (i % 4) * 128)
                        mm(p1_, lt, hh[:, k, :], k == 0, k == 7, [wk_] + gk(T, k), [k1_])
                    if i % 2 == 0:
                        act(h1[:, i, :], p1_, AF.Relu, [k1_], gk(g1, i))
                    else:
                        tsc('dve', h1[:, i, :], p1_, 0.0, None, ALU.max, None, [k1_], gk(g1, i))
                    tt('pool', h1[:, i, :], h1[:, i, :], h1[:, i, :], ALU.mult, gk(g1, i), gk(g1, i))
                if T == NT - 1:
                    for q in range(2):
                        release(('w1', l, fb * 2 + q))

            def mD(s):
                fb, T = s // NT, s % NT
                g1 = 4 + (s % 2)
                h1 = g3(g1)
                for j in range(NCH):
                    k2_, p2_ = bank()
                    for i in range(NCH):
                        s_ = wslot(('w2', l, fb * 2 + i // 4))
                        w2u = wsl[:, s_, :].rearrange("p (k n) -> p k n", k=4)
                        mm(p2_, w2u[:, i % 4, j * 128:(j + 1) * 128], h1[:, i, :], i == 0, i == 7,
                           [('w', s_)] + gk(g1, i), [k2_])
                    tt('dve', xa_(j, T), p2_, xa_(j, T), ALU.add, [k2_, ('x', j, T)], [('x', j, T)])
                if T == NT - 1:
                    for q in range(2):
                        release(('w2', l, fb * 2 + q))

            mNb(0)
            mNa(2)
            mNb(1)
            mNa(3)
            mU(0)
            mNb(2)
            mNb(3)
            for s in range(4 * NT):
                if s + 1 < 4 * NT:
                    mU(s + 1)
                mD(s)

        def fNa(T):
            act(g3(5), xs(T), AF.Square, xkeys(T), gk(5))

        def fNb(T):
            sq = g3(5)
            bk, bp = bank()
            for c in range(NCH):
                mm(bp, onesb, sq[:, c, :], c == 0, c == 7, gk(5, c) + [('onesb',)], [bk])
            k1, s1 = stat()
            rstd_from(bp, TT, k1, s1, [bk], epsr)
            for c in range(NCH):
                stt(xa_(c, T), xa_(c, T), vcol(c, R_FIN), s1, ALU.mult, ALU.mult,
                    [('x', c, T), k1, ('vecs',)], [('x', c, T)])

        nout = [0]

        def fO(T):
            for s4 in range(4):
                t0 = T * TT + s4 * 128
                sap, skeys, dk = gstage(nout[0] % 10)
                for half in range(2):
                    bk, bp = bank()
                    for cc in range(4):
                        c = half * 4 + cc
                        P.op('pe', lambda e, c=c, cc=cc, bp=bp, t0=t0: e.transpose(
                            bp[:, cc * 128:(cc + 1) * 128], xT[:, c, t0:t0 + 128], identf),
                            [('x', c, T), ('identf',)], [bk])
                    copy('act' if half == 0 else 'dve', sap[:, half * 512:(half + 1) * 512], bp,
                         [bk], skeys[half * 2:half * 2 + 2])
                P.dma('sp', [lambda e, sap=sap, t0=t0: e.dma_start(out=out_d[t0:t0 + 128, :], in_=sap)],
                      dk, skeys, [('out', nout[0])])
                nout[0] += 1

        if final:
            fNa(0)
            fNb(0)
            for T in range(NT):
                if T + 1 < NT:
                    fNa(T + 1)
                fO(T)
                if T + 1 < NT:
                    fNb(T + 1)
        else:
            for T in range(NT):
                fO(T)
        nout = nout[0]
        P.wait('sp', [('out', i) for i in range(nout)])
        P.emit(nc, stack)
    return nc


LAUNCH_GROUPS = [(0, 4)]
_CACHE = {}


def _prog(l0, l1, final):
    key = (l0, l1, final)
    if key not in _CACHE:
        _CACHE[key] = build(l0, l1, final)
    return _CACHE[key]


def kernel(**inp):
    f = lambda a: np.ascontiguousarray(np.asarray(a, dtype=np.float32))
    B = 8
    x = f(inp["x"])
    mem = f(inp["mem"])
    shared = {
        "norm_mix": f(inp["norm_mix"]), "norm_xattn": f(inp["norm_xattn"]), "norm_mlp": f(inp["norm_mlp"]),
        "conv_b_in": f(inp["conv_b_in"]).reshape(4, D), "conv_b_dw": f(inp["conv_b_dw"]),
        "conv_ln_g": f(inp["conv_ln_g"]), "conv_ln_b": f(inp["conv_ln_b"]), "conv_b_out": f(inp["conv_b_out"]),
        "pool_scale": f(inp["pool_scale"]), "mem_norm": f(inp["mem_norm"]).reshape(1, D),
        "final_norm": f(inp["final_norm"]).reshape(1, D), "conv_w_dw": f(inp["conv_w_dw"]).reshape(2 * CW, D),
        "conv_w_in": f(inp["conv_w_in"]), "conv_w_out": f(inp["conv_w_out"]), "pool_w": f(inp["pool_w"]),
        "xattn_wq": f(inp["xattn_wq"]), "xattn_wk": f(inp["xattn_wk"]), "xattn_wv": f(inp["xattn_wv"]),
        "xattn_wo": f(inp["xattn_wo"]), "mlp_w1": f(inp["mlp_w1"]), "mlp_w2": f(inp["mlp_w2"]),
    }
    cur = [x[b] for b in range(B)]
    for (l0, l1) in LAUNCH_GROUPS:
        nc = _prog(l0, l1, l1 == DEPTH)
        in_maps = [dict(shared, x=cur[b], mem=mem[b]) for b in range(B)]
        res = run_bass_kernel_spmd(nc, in_maps, core_ids=list(range(B)))
        cur = [np.asarray(res.results[b]["out"], dtype=np.float32) for b in range(B)]
    return np.stack(cur, axis=0)
```
